# Optimizing a Trainium2 kernel written in Bass

```python
import math
import jax
import jax.numpy as jnp
from jax import lax
import numpy as np

D_MODEL = 1024
BATCH = 8
SEQ = 4096
DEPTH = 4

GRID_W = 64
CTX_LEN = 256
N_MOD = 6
N_BRANCH = 3
NORM_EPS = 1e-6
RNN_WIDTH = D_MODEL
RNN_BLOCKS = 16
RNN_BLOCK = RNN_WIDTH // RNN_BLOCKS
RNN_CONV = 4
RNN_PAD = (2, 1)
RNN_C = 8.0
ATTN_HEADS = 8
HEAD_DIM = 64
ATTN_WIDTH = ATTN_HEADS * 2 * HEAD_DIM
ROPE_PAIRS_AXIS = HEAD_DIM // 4
ROPE_BASE = 10000.0
Q_BLOCK = 128
CONV_WIDTH = D_MODEL
CONV_K = 31
CONV_PAD = ((CONV_K - 1) // 2, (CONV_K - 1) // 2)
D_FF = -(-8 * D_MODEL // (3 * 256)) * 256
OFF_RX = 0
OFF_K = OFF_RX + RNN_WIDTH
OFF_V = OFF_K + ATTN_WIDTH
CTX_COLS = OFF_V + ATTN_WIDTH
OFF_RG = CTX_COLS
OFF_Q = OFF_RG + RNN_WIDTH
OFF_CV = OFF_Q + ATTN_WIDTH
OFF_CG = OFF_CV + CONV_WIDTH
OFF_G = OFF_CG + CONV_WIDTH
IN_COLS = OFF_G + N_BRANCH * D_MODEL

kernel_name = 'hybrid_rglru_diffattn_conformer_dit'


def rmsnorm(x, g):
    xf = x.astype(jnp.float32)
    y = xf * lax.rsqrt(jnp.mean(xf * xf, axis=-1, keepdims=True) + NORM_EPS)
    return (y * g.astype(jnp.float32)).astype(x.dtype)


def layernorm(x, g, b):
    xf = x.astype(jnp.float32)
    mu = jnp.mean(xf, axis=-1, keepdims=True)
    var = jnp.mean(jnp.square(xf - mu), axis=-1, keepdims=True)
    y = (xf - mu) * lax.rsqrt(var + NORM_EPS)
    return (y * g.astype(jnp.float32) + b.astype(jnp.float32)).astype(x.dtype)


def modulate(h, shift, scale):
    return h * (1 + scale) + shift


def depthwise_conv(x, w, b, pad):
    y = lax.conv_general_dilated(x, w[:, None, :].astype(x.dtype), window_strides=(1,),
                                 padding=[pad], dimension_numbers=('NWC', 'WIO', 'NWC'),
                                 feature_group_count=x.shape[-1])
    return y + b.astype(x.dtype)


def blockdiag(x, w, b):
    B, L, _ = x.shape
    xr = x.reshape(B, L, RNN_BLOCKS, RNN_BLOCK)
    y = jnp.einsum('blni,nio->blno', xr, w.astype(jnp.float32))
    return y.reshape(B, L, RNN_WIDTH) + b.astype(jnp.float32)


def rglru_coeffs(xc, w_a, b_a, w_x, b_x, lam):
    xf = xc.astype(jnp.float32)
    r = jax.nn.sigmoid(blockdiag(xf, w_a, b_a))
    i = jax.nn.sigmoid(blockdiag(xf, w_x, b_x))
    log_a = -RNN_C * r * jax.nn.softplus(-lam.astype(jnp.float32))
    a = jnp.exp(log_a)
    mult = jnp.sqrt(-jnp.expm1(2.0 * log_a))
    return a, mult * (i * xf)


def _scan_combine(left, right):
    a1, b1 = left
    a2, b2 = right
    return a1 * a2, a2 * b1 + b2


def linear_recurrence(a, b, h0, reverse):
    if reverse:
        a, b = jnp.flip(a, 1), jnp.flip(b, 1)
    if h0 is not None:
        b = b.at[:, 0].add(a[:, 0] * h0)
    _, h = lax.associative_scan(_scan_combine, (a, b), axis=1)
    if reverse:
        h = jnp.flip(h, 1)
    return h


def rglru_bidir(xr_c, xr, w_a, b_a, w_x, b_x, lam, need_ctx_out):
    ys, ys_c = [], []
    for d in range(2):
        rev = d == 1
        a_c, b_c = rglru_coeffs(xr_c, w_a[d], b_a[d], w_x[d], b_x[d], lam[d])
        h_c = linear_recurrence(a_c, b_c, None, rev)
        h_end = h_c[:, 0] if rev else h_c[:, -1]
        a, bb = rglru_coeffs(xr, w_a[d], b_a[d], w_x[d], b_x[d], lam[d])
        ys.append(linear_recurrence(a, bb, h_end, rev))
        ys_c.append(h_c)
    y_c = ys_c[0] + ys_c[1] if need_ctx_out else None
    return y_c, ys[0] + ys[1]


def axial_rope(rows):
    r = jnp.repeat(jnp.arange(rows, dtype=jnp.float32), GRID_W)
    col = jnp.tile(jnp.arange(GRID_W, dtype=jnp.float32), rows)
    inv = ROPE_BASE ** (-jnp.arange(ROPE_PAIRS_AXIS, dtype=jnp.float32) / ROPE_PAIRS_AXIS)
    ang = jnp.concatenate([r[:, None] * inv, col[:, None] * inv], axis=-1)
    return jnp.cos(ang), jnp.sin(ang)


def apply_rope(t, cos, sin):
    half = HEAD_DIM // 2
    t1, t2 = t[..., :half], t[..., half:]
    cs = cos[None, :, None, None, :].astype(t.dtype)
    sn = sin[None, :, None, None, :].astype(t.dtype)
    return jnp.concatenate([t1 * cs - t2 * sn, t1 * sn + t2 * cs], axis=-1)


def diff_softmax_mix(q, k, v, lam):
    s = jnp.einsum('bqhmd,bkhmd->bhmqk', q, k).astype(jnp.float32) * (HEAD_DIM ** -0.5)
    p = jax.nn.softmax(s, axis=-1)
    w = p[:, :, 0] - lam.astype(jnp.float32) * p[:, :, 1]
    return jnp.einsum('bhqk,bkhe->bqhe', w, v.astype(jnp.float32))


def latent_diff_attention(q, k_all, v_all, lam):
    B, S = q.shape[0], q.shape[1]
    nb = S // Q_BLOCK
    qb = jnp.moveaxis(q.reshape(B, nb, Q_BLOCK, ATTN_HEADS, 2, HEAD_DIM), 1, 0)
    o = lax.map(lambda qq: diff_softmax_mix(qq, k_all, v_all, lam), qb)
    return jnp.moveaxis(o, 0, 1).reshape(B, S, ATTN_HEADS, 2 * HEAD_DIM)


def attn_post(o, g, lam_init, w_o, dtype):
    o = rmsnorm(o, g) * (1.0 - lam_init)
    return o.reshape(o.shape[0], o.shape[1], ATTN_WIDTH).astype(dtype) @ w_o


def conformer_conv(val, gate, dw_w, dw_b, ln_g, ln_b, w_o):
    z = val * jax.nn.sigmoid(gate)
    z = depthwise_conv(z, dw_w, dw_b, CONV_PAD)
    z = jax.nn.silu(layernorm(z, ln_g, ln_b))
    return z @ w_o


def gated_merge(gate_logits, y_r, y_a, y_c, w_o):
    g = jax.nn.sigmoid(gate_logits)
    m = (g[..., :D_MODEL] * y_r + g[..., D_MODEL:2 * D_MODEL] * y_a
         + g[..., 2 * D_MODEL:] * y_c)
    return m @ w_o


def swiglu(u, w_i, w_o):
    gu = u @ w_i
    return (jax.nn.silu(gu[..., :D_FF]) * gu[..., D_FF:]) @ w_o


def setup_inputs(seed: int = 0) -> dict:
    key = jax.random.key(seed)
    ks = jax.random.split(key, 30)

    def nrm(k, shape, scale):
        return scale * jax.random.normal(k, shape, jnp.float32)

    L = DEPTH
    u = jax.random.uniform(ks[16], (L, 2, RNN_WIDTH), jnp.float32, minval=0.9, maxval=0.999)
    a0 = u ** (1.0 / RNN_C)
    return {
        'x': nrm(ks[0], (BATCH, SEQ, D_MODEL), 1.0),
        'c': nrm(ks[1], (BATCH, D_MODEL), 1.0),
        'ctx': nrm(ks[2], (BATCH, CTX_LEN, D_MODEL), 1.0),
        'c_ctx': nrm(ks[3], (D_MODEL,), 1.0),
        'w_mod': nrm(ks[4], (L, D_MODEL, N_MOD * D_MODEL), 0.5 * D_MODEL ** -0.5),
        'b_mod': nrm(ks[5], (L, N_MOD * D_MODEL), 0.02),
        'g_norm1': 1.0 + nrm(ks[6], (L, D_MODEL), 0.02),
        'g_norm2': 1.0 + nrm(ks[7], (L, D_MODEL), 0.02),
        'w_in': nrm(ks[8], (L, D_MODEL, IN_COLS), D_MODEL ** -0.5),
        'b_in': nrm(ks[9], (L, IN_COLS), 0.02),
        'rnn_conv_w': nrm(ks[10], (L, RNN_CONV, RNN_WIDTH), RNN_CONV ** -0.5),
        'rnn_conv_b': nrm(ks[11], (L, RNN_WIDTH), 0.02),
        'rnn_w_a': nrm(ks[12], (L, 2, RNN_BLOCKS, RNN_BLOCK, RNN_BLOCK), RNN_BLOCK ** -0.5),
        'rnn_b_a': nrm(ks[13], (L, 2, RNN_WIDTH), 0.02),
        'rnn_w_x': nrm(ks[14], (L, 2, RNN_BLOCKS, RNN_BLOCK, RNN_BLOCK), RNN_BLOCK ** -0.5),
        'rnn_b_x': nrm(ks[15], (L, 2, RNN_WIDTH), 0.02),
        'rnn_lambda': jnp.log(a0) - jnp.log1p(-a0),
        'w_rnn_o': nrm(ks[17], (L, RNN_WIDTH, D_MODEL), RNN_WIDTH ** -0.5),
        'lambda_qk': nrm(ks[18], (L, 4, HEAD_DIM), 0.1),
        'g_subln': 1.0 + nrm(ks[19], (L, 2 * HEAD_DIM), 0.02),
        'w_attn_o': nrm(ks[20], (L, ATTN_WIDTH, D_MODEL), ATTN_WIDTH ** -0.5),
        'conv_dw_w': nrm(ks[21], (L, CONV_K, CONV_WIDTH), CONV_K ** -0.5),
        'conv_dw_b': nrm(ks[22], (L, CONV_WIDTH), 0.02),
        'conv_ln_g': 1.0 + nrm(ks[23], (L, CONV_WIDTH), 0.02),
        'conv_ln_b': nrm(ks[24], (L, CONV_WIDTH), 0.02),
        'w_conv_o': nrm(ks[25], (L, CONV_WIDTH, D_MODEL), CONV_WIDTH ** -0.5),
        'w_out': nrm(ks[26], (L, D_MODEL, D_MODEL), D_MODEL ** -0.5),
        'w_ffn_in': nrm(ks[27], (L, D_MODEL, 2 * D_FF), D_MODEL ** -0.5),
        'w_ffn_out': nrm(ks[28], (L, D_FF, D_MODEL), D_FF ** -0.5),
        'g_final': 1.0 + nrm(ks[29], (D_MODEL,), 0.02),
    }


def reference(x, c, ctx, c_ctx, w_mod, b_mod, g_norm1, g_norm2, w_in, b_in,
              rnn_conv_w, rnn_conv_b, rnn_w_a, rnn_b_a, rnn_w_x, rnn_b_x, rnn_lambda, w_rnn_o,
              lambda_qk, g_subln, w_attn_o,
              conv_dw_w, conv_dw_b, conv_ln_g, conv_ln_b, w_conv_o,
              w_out, w_ffn_in, w_ffn_out, g_final):
    dt = x.dtype
    B, S = x.shape[0], x.shape[1]
    CL = ctx.shape[1]
    rows = S // GRID_W
    cos, sin = axial_rope(rows)
    s_c = jax.nn.silu(c)
    s_cc = jax.nn.silu(c_ctx)
    h, hc = x, ctx
    for l in range(DEPTH):
        ctx_out = l < DEPTH - 1
        lam_init = 0.8 - 0.6 * math.exp(-0.3 * l)
        mod = (s_c @ w_mod[l] + b_mod[l])[:, None, :]
        mod_c = (s_cc @ w_mod[l] + b_mod[l])[None, None, :]
        sh1, sc1, ga1, sh2, sc2, ga2 = jnp.split(mod, N_MOD, axis=-1)
        csh1, csc1, cga1, csh2, csc2, cga2 = jnp.split(mod_c, N_MOD, axis=-1)

        u = modulate(rmsnorm(h, g_norm1[l]), sh1, sc1)
        uc = modulate(rmsnorm(hc, g_norm1[l]), csh1, csc1)
        p = u @ w_in[l] + b_in[l]
        ncol = IN_COLS if ctx_out else CTX_COLS
        pc = uc @ w_in[l][:, :ncol] + b_in[l][:ncol]

        xr = depthwise_conv(p[..., OFF_RX:OFF_K], rnn_conv_w[l], rnn_conv_b[l], RNN_PAD)
        xr_c = depthwise_conv(pc[..., OFF_RX:OFF_K], rnn_conv_w[l], rnn_conv_b[l], RNN_PAD)
        hr_c, hr = rglru_bidir(xr_c, xr, rnn_w_a[l], rnn_b_a[l], rnn_w_x[l], rnn_b_x[l],
                               rnn_lambda[l], ctx_out)
        y_r = (hr.astype(dt) * jax.nn.gelu(p[..., OFF_RG:OFF_Q], approximate=True)) @ w_rnn_o[l]

        q = apply_rope(p[..., OFF_Q:OFF_CV].reshape(B, S, ATTN_HEADS, 2, HEAD_DIM), cos, sin)
        k = apply_rope(p[..., OFF_K:OFF_V].reshape(B, S, ATTN_HEADS, 2, HEAD_DIM), cos, sin)
        v = p[..., OFF_V:CTX_COLS].reshape(B, S, ATTN_HEADS, 2 * HEAD_DIM)
        kc = pc[..., OFF_K:OFF_V].reshape(B, CL, ATTN_HEADS, 2, HEAD_DIM)
        vc = pc[..., OFF_V:CTX_COLS].reshape(B, CL, ATTN_HEADS, 2 * HEAD_DIM)
        lq = lambda_qk[l]
        lam = jnp.exp(jnp.sum(lq[0] * lq[1])) - jnp.exp(jnp.sum(lq[2] * lq[3])) + lam_init
        o = latent_diff_attention(q, jnp.concatenate([kc, k], axis=1),
                                  jnp.concatenate([vc, v], axis=1), lam)
        y_a = attn_post(o, g_subln[l], lam_init, w_attn_o[l], dt)

        y_c = conformer_conv(p[..., OFF_CV:OFF_CG], p[..., OFF_CG:OFF_G], conv_dw_w[l], conv_dw_b[l],
                             conv_ln_g[l], conv_ln_b[l], w_conv_o[l])

        h = h + ga1 * gated_merge(p[..., OFF_G:], y_r, y_a, y_c, w_out[l])
        h = h + ga2 * swiglu(modulate(rmsnorm(h, g_norm2[l]), sh2, sc2), w_ffn_in[l], w_ffn_out[l])

        if ctx_out:
            qc = pc[..., OFF_Q:OFF_CV].reshape(B, CL, ATTN_HEADS, 2, HEAD_DIM)
            yr_c = (hr_c.astype(dt) * jax.nn.gelu(pc[..., OFF_RG:OFF_Q], approximate=True)) @ w_rnn_o[l]
            ya_c = attn_post(diff_softmax_mix(qc, kc, vc, lam), g_subln[l], lam_init, w_attn_o[l], dt)
            yc_c = conformer_conv(pc[..., OFF_CV:OFF_CG], pc[..., OFF_CG:OFF_G], conv_dw_w[l],
                                  conv_dw_b[l], conv_ln_g[l], conv_ln_b[l], w_conv_o[l])
            hc = hc + cga1 * gated_merge(pc[..., OFF_G:], yr_c, ya_c, yc_c, w_out[l])
            hc = hc + cga2 * swiglu(modulate(rmsnorm(hc, g_norm2[l]), csh2, csc2),
                                    w_ffn_in[l], w_ffn_out[l])
    return rmsnorm(h, g_final)
```

```python
import contextlib
import math
import numpy as np
import concourse.bass as bass
import concourse.mybir as mybir
from concourse.bass_utils import run_bass_kernel_spmd

F32 = mybir.dt.float32
BF16 = mybir.dt.bfloat16
AF = mybir.ActivationFunctionType
ALU = mybir.AluOpType

D = 1024
SEQ = 4096
CL = 256
T = CL + SEQ
DEPTH = 4
NCORES = 8
EPS = 1e-6
DFF = 2816
NFC = DFF // 128
OFF_RX, OFF_K, OFF_V, OFF_RG, OFF_Q, OFF_CV, OFF_CG, OFF_G = 0, 1024, 2048, 3072, 4096, 5120, 6144, 7168
IN_COLS = 10240

PP_FIELDS = [("b_in", 80), ("g1", 8), ("g2", 8), ("rcw", 32), ("rcb", 8), ("rba", 16), ("rbx", 16),
             ("rlam", 16), ("cdw", 248), ("cdb", 8), ("clg", 8), ("clb", 8), ("bmod", 48), ("gsub", 1)]
PP_OFF = {}
_o = 0
for _n, _c in PP_FIELDS:
    PP_OFF[_n] = _o
    _o += _c
NPP = _o


class Sched:
    ENGS = ("pe", "act", "dve", "pool", "sp")

    def __init__(self, nc):
        self.nc = nc
        self.q = {e: [] for e in self.ENGS}
        self.cnt = {}
        self.waited = {}
        self.last_w = {}
        self.readers = {}
        self.sems = {}

    def _sem(self, key):
        if key not in self.sems:
            self.sems[key] = self.nc.alloc_semaphore(name="s_%d" % len(self.sems))
            self.cnt[key] = 0
        return self.sems[key]

    def _deps(self, eng, reads, writes):
        deps = {}
        for r in reads:
            d = self.last_w.get(r)
            if d is not None and deps.get(d[0], 0) < d[1]:
                deps[d[0]] = d[1]
        for w in writes:
            d = self.last_w.get(w)
            if d is not None and deps.get(d[0], 0) < d[1]:
                deps[d[0]] = d[1]
            rd = self.readers.get(w)
            if rd:
                for k, v in rd.items():
                    if deps.get(k, 0) < v:
                        deps[k] = v
        out = []
        for k, v in deps.items():
            if k == ("eng", "pe") and eng == "pe":
                continue
            if self.waited.get((eng, k), 0) < v:
                self.waited[(eng, k)] = v
                out.append((self._sem(k), v))
        return out

    def _commit(self, semkey, val, reads, writes):
        for r in reads:
            self.readers.setdefault(r, {})[semkey] = val
        for w in writes:
            self.last_w[w] = (semkey, val)
            self.readers[w] = {}

    def op(self, eng, fn, reads=(), writes=()):
        waits = self._deps(eng, reads, writes)
        semkey = ("eng", eng)
        sem = self._sem(semkey)
        self.cnt[semkey] += 1
        self.q[eng].append((waits, fn, sem, 1))
        self._commit(semkey, self.cnt[semkey], reads, writes)

    def dma(self, eng, fn, semkey, reads=(), writes=()):
        waits = self._deps(eng, reads, writes)
        semkey = ("dma", semkey)
        sem = self._sem(semkey)
        self.cnt[semkey] += 16
        self.q[eng].append((waits, fn, sem, 16))
        self._commit(semkey, self.cnt[semkey], reads, writes)

    def barrier(self):
        for eng in self.ENGS:
            waits = []
            for k, v in self.cnt.items():
                if v > 0 and self.waited.get((eng, k), 0) < v and k != ("eng", eng):
                    self.waited[(eng, k)] = v
                    waits.append((self._sem(k), v))
            if waits:
                self.q[eng].append((waits, None, None, 0))
        self.last_w = {}
        self.readers = {}

    def emit(self, final_eng="sp"):
        nc = self.nc
        waits = [(self._sem(k), v) for k, v in self.cnt.items() if v > 0 and k != ("eng", final_eng)]
        self.q[final_eng].append((waits, None, None, 0))
        hmap = {"pe": "tensor", "act": "scalar", "dve": "vector", "pool": "gpsimd", "sp": "sync"}
        with nc.Block() as block:
            for eng in self.ENGS:
                def body(e, items=self.q[eng]):
                    for waits, fn, sem, inc in items:
                        for s, v in waits:
                            e.wait_ge(s, v)
                        if fn is not None:
                            fn(e).then_inc(sem, inc)
                getattr(block, hmap[eng])(body)


class Prog:
    def __init__(self, n_layers=DEPTH, debug=False, nphase=99):
        self.nphase = nphase
        self.L = n_layers
        self.debug = debug
        self.nc = bass.Bass("TRN2", target_bir_lowering=False)
        self.S = Sched(self.nc)
        self.uid = 0

    def dram(self, name, shape, dt, kind="Internal"):
        return self.nc.dram_tensor(name, list(shape), dt, kind=kind).ap()

    def mm(self, out, lhsT, rhs, start, stop, reads, writes):
        self.S.op("pe", lambda e: e.matmul(out, lhsT, rhs, start=start, stop=stop), reads, writes)

    def act(self, out, in_, func, reads, writes, bias=0.0, scale=1.0, accum_out=None):
        if accum_out is None:
            self.S.op("act", lambda e: e.activation(out=out, in_=in_, func=func, bias=bias, scale=scale), reads, writes)
        else:
            self.S.op("act", lambda e: e.activation(out=out, in_=in_, func=func, bias=bias, scale=scale,
                                                    accum_out=accum_out), reads, writes)

    def tt(self, eng, out, in0, in1, op, reads, writes):
        self.S.op(eng, lambda e: e.tensor_tensor(out=out, in0=in0, in1=in1, op=op), reads, writes)

    def ts(self, eng, out, in0, s1, op0, reads, writes, s2=None, op1=None):
        if op1 is None:
            self.S.op(eng, lambda e: e.tensor_scalar(out=out, in0=in0, scalar1=s1, scalar2=None, op0=op0), reads, writes)
        else:
            self.S.op(eng, lambda e: e.tensor_scalar(out=out, in0=in0, scalar1=s1, scalar2=s2, op0=op0, op1=op1), reads, writes)

    def stt(self, out, in0, scalar, in1, op0, op1, reads, writes):
        self.S.op("dve", lambda e: e.scalar_tensor_tensor(out=out, in0=in0, scalar=scalar, in1=in1, op0=op0, op1=op1),
                  reads, writes)

    def copy(self, eng, out, in_, reads, writes):
        if eng == "act":
            self.S.op("act", lambda e: e.copy(out=out, in_=in_), reads, writes)
        else:
            self.S.op(eng, lambda e: e.tensor_copy(out=out, in_=in_), reads, writes)

    def recip(self, out, in_, reads, writes):
        self.S.op("dve", lambda e: e.reciprocal(out=out, in_=in_), reads, writes)

    def memset(self, eng, ap, val, writes):
        self.S.op(eng, lambda e: e.memset(ap, val), (), writes)

    def dma(self, eng, out, in_, key, reads, writes):
        self.S.dma(eng, lambda e: e.dma_start(out=out, in_=in_), key, reads, writes)

    def build(self):
        nc = self.nc
        L = self.L
        In = "ExternalInput"
        self.xin = self.dram("xin", [D, T], F32, In)
        self.cvec = self.dram("cvec", [128, 8, 2], F32, In)
        self.pp_d = self.dram("pp", [128, DEPTH, NPP], F32, In)
        self.gfin_d = self.dram("gfin", [128, 8], F32, In)
        self.bv_d = self.dram("bv", [DEPTH, 1024], F32, In)
        self.lq_d = self.dram("lq", [1, DEPTH * 256], F32, In)
        self.cos_d = self.dram("ropec", [128, SEQ], F32, In)
        self.sin_d = self.dram("ropes", [128, SEQ], F32, In)
        self.cst_d = self.dram("cst", [128, 3, 128], F32, In)
        self.w_mod = self.dram("w_mod", [DEPTH, D, 6 * D], F32, In)
        self.w_in = self.dram("w_in", [DEPTH, D, IN_COLS], F32, In)
        self.rnn_w_a = self.dram("rnn_w_a", [DEPTH, 2, 16, 64, 64], F32, In)
        self.rnn_w_x = self.dram("rnn_w_x", [DEPTH, 2, 16, 64, 64], F32, In)
        self.w_rnn_o = self.dram("w_rnn_o", [DEPTH, D, D], F32, In)
        self.w_attn_o = self.dram("w_attn_o", [DEPTH, D, D], F32, In)
        self.w_conv_o = self.dram("w_conv_o", [DEPTH, D, D], F32, In)
        self.w_out = self.dram("w_out", [DEPTH, D, D], F32, In)
        self.w_ffn_in = self.dram("w_ffn_in", [DEPTH, D, 2 * DFF], F32, In)
        self.w_ffn_out = self.dram("w_ffn_out", [DEPTH, DFF, D], F32, In)
        self.out_d = self.dram("out", [D, SEQ], F32, "ExternalOutput")
        dbg = "ExternalOutput" if self.debug else "Internal"
        self.hT = self.dram("hT", [D, T], F32, dbg)
        self.rxT = self.dram("rxT", [D, T], BF16, dbg)
        self.rgT = self.dram("rgT", [D, T], BF16, dbg)
        self.kT = self.dram("kT", [D, T], BF16, dbg)
        self.qT = self.dram("qT", [D, T], BF16, dbg)
        self.zT = self.dram("zT", [D, T], BF16, dbg)
        self.zcT = self.dram("zcT", [D, T], BF16, dbg)
        self.trT = self.dram("trT", [D, T], BF16, dbg)
        self.oT = self.dram("oT", [D, T], BF16, dbg)
        self.gT = self.dram("gT", [3 * D, T], BF16, dbg)
        self.Vs = self.dram("Vs", [T, D], BF16, dbg)
        self.u2T = self.dram("u2T", [D, T], BF16, "Internal")
        self.u1T = self.dram("u1T", [D, T], BF16, "Internal")
        self.wm_bf = self.dram("wm_bf", [4, D, D], BF16, "Internal")
        self.w1_bf = self.dram("w1_bf", [D, 2 * DFF], BF16, "Internal")
        self.w2_bf = self.dram("w2_bf", [DFF, D], BF16, "Internal")

        with contextlib.ExitStack() as top:
            def sb(name, shape, dt):
                return top.enter_context(nc.sbuf_tensor(name, list(shape), dt))
            self.ps = [top.enter_context(nc.psum_tensor("ps%d" % i, [128, 512], F32)) for i in range(7)]
            self.psb = top.enter_context(nc.psum_tensor("psb", [128, 1024], BF16))
            self.pp = sb("pp_sb", [128, DEPTH, NPP], F32)
            self.cstf = sb("cstf", [128, 3, 128], F32)
            self.cstb = sb("cstb", [128, 3, 128], BF16)
            self.mod = sb("mod", [128, DEPTH, 48, 2], F32)
            self.gsc1 = sb("gsc1", [128, DEPTH, 8, 2], F32)
            self.gsc2 = sb("gsc2", [128, DEPTH, 8, 2], F32)
            self.cdec = sb("cdec", [128, DEPTH, 16], F32)
            self.nlam = sb("nlam", [128, DEPTH], F32)
            self.gsub = sb("gsubs", [128, DEPTH], F32)
            self.gfin = sb("gfin_sb", [128, 8], F32)
            self.prologue()
            np_ = 0
            for l in range(L):
                for ph in (self.phase_proj, self.phase_rc, self.phase_attn, self.phase_merge, self.phase_ffn):
                    if np_ < self.nphase:
                        self.S.barrier()
                        ph(l)
                    np_ += 1
            if np_ <= self.nphase and L < DEPTH:
                self.S.barrier()
                self.phase_final()
            self.S.emit()
        return nc

    def ppc(self, l, name, idx=0):
        o = PP_OFF[name] + idx
        return self.pp[:, l, o:o + 1]

    def prologue(self):
        nc, S = self.nc, self.S
        self.dma("sp", self.pp[:], self.pp_d, "pp", [], ["pp"])
        self.dma("sp", self.cstf[:], self.cst_d, "cst", [], ["cstf"])
        self.dma("sp", self.gfin[:], self.gfin_d, "gfin", [], ["gfin"])
        self.copy("dve", self.cstb[:], self.cstf[:], ["cstf"], ["cstb"])
        with contextlib.ExitStack() as es:
            def sb(name, shape, dt):
                return es.enter_context(nc.sbuf_tensor(name, list(shape), dt))
            cv = sb("p_cv", [128, 8, 2], F32)
            sc = sb("p_sc", [128, 8, 2], F32)
            wm = [sb("p_wm%d" % i, [128, 8, 768], F32) for i in range(2)]
            lqb = sb("p_lq", [128, DEPTH, 256], F32)
            lt = sb("p_lt", [128, 2, 64], F32)
            ls = sb("p_ls", [128, DEPTH, 2], F32)
            tmp16 = sb("p_t16", [128, DEPTH, 16], F32)
            self.dma("sp", cv[:], self.cvec, "cv", [], ["cv"])
            self.act(sc[:], cv[:], AF.Silu, ["cv"], ["sc"])
            it = 0
            for l in range(self.L):
                for g in range(8):
                    slot = it % 2
                    it += 1
                    src = self.w_mod[l, :, g * 768:(g + 1) * 768].rearrange("(kc p) n -> p kc n", p=128)
                    self.dma("sp", wm[slot][:], src, ("wm", slot), [], [("wm", slot)])
                    for cc in range(6):
                        ch = g * 6 + cc
                        pst = self.ps[cc % 4]
                        for kc in range(8):
                            self.mm(pst[:, 0:2], wm[slot][:, kc, cc * 128:(cc + 1) * 128], sc[:, kc, :],
                                    kc == 0, kc == 7, [("wm", slot), "sc"], [("ps", cc % 4)])
                        self.ts("dve", self.mod[:, l, ch, :], pst[:, 0:2], self.ppc(l, "bmod", ch), ALU.add,
                                [("ps", cc % 4), "pp"], [("mod", l)])
            for l in range(self.L):
                for v in range(2):
                    for (dst, gname, base) in ((self.gsc1, "g1", 8), (self.gsc2, "g2", 32)):
                        o = PP_OFF[gname]
                        self.stt(dst[:, l, :, v], self.mod[:, l, base:base + 8, v], 1.0, self.pp[:, l, o:o + 8],
                                 ALU.add, ALU.mult, [("mod", l), "pp"], [("gsc", l)])
            for l in range(self.L):
                o = PP_OFF["rlam"]
                self.act(tmp16[:, l, :], self.pp[:, l, o:o + 16], AF.Exp, ["pp"], [("t16", l)], scale=-1.0)
                self.act(tmp16[:, l, :], tmp16[:, l, :], AF.Ln, [("t16", l)], [("t16", l)], bias=1.0)
                self.ts("dve", self.cdec[:, l, :], tmp16[:, l, :], -8.0, ALU.mult, [("t16", l)], [("cdec", l)])
            self.dma("sp", lqb[:].rearrange("p l n -> p (l n)"),
                     self.lq_d.partition_broadcast(128), "lq", [], ["lq"])
            for l in range(self.L):
                lam_init = 0.8 - 0.6 * math.exp(-0.3 * l)
                for i in range(2):
                    self.tt("dve", lt[:, i, :], lqb[:, l, (2 * i) * 64:(2 * i + 1) * 64],
                            lqb[:, l, (2 * i + 1) * 64:(2 * i + 2) * 64], ALU.mult, ["lq"], ["lt"])
                    self.S.op("dve", lambda e, i=i, l=l: e.tensor_reduce(out=ls[:, l, i:i + 1], in_=lt[:, i, :],
                                                                          axis=mybir.AxisListType.X, op=ALU.add),
                              ["lt"], [("ls", l)])
                self.act(ls[:, l, :], ls[:, l, :], AF.Exp, [("ls", l)], [("ls", l)])
                self.tt("dve", self.nlam[:, l:l + 1], ls[:, l, 1:2], ls[:, l, 0:1], ALU.subtract, [("ls", l)], [("nlam", l)])
                self.ts("dve", self.nlam[:, l:l + 1], self.nlam[:, l:l + 1], -lam_init, ALU.add, [("nlam", l)], [("nlam", l)])
                self.ts("dve", self.gsub[:, l:l + 1], self.ppc(l, "gsub"), 1.0 - lam_init, ALU.mult, ["pp"], [("gsub", l)])
            if self.debug:
                d_mod = self.dram("d_mod", [128, DEPTH * 48 * 2], F32, "ExternalOutput")
                d_misc = self.dram("d_misc", [128, DEPTH * 16 + 2 * DEPTH], F32, "ExternalOutput")
                rd = [("mod", l) for l in range(self.L)] + [("cdec", l) for l in range(self.L)] + \
                     [("nlam", l) for l in range(self.L)] + [("gsub", l) for l in range(self.L)]
                self.dma("sp", d_mod, self.mod[:].rearrange("p a b c -> p (a b c)"), "d1", rd, [])
                self.dma("sp", d_misc[:, 0:DEPTH * 16], self.cdec[:].rearrange("p a b -> p (a b)"), "d2", rd, [])
                self.dma("sp", d_misc[:, DEPTH * 16:DEPTH * 17], self.nlam[:], "d3", rd, [])
                self.dma("sp", d_misc[:, DEPTH * 17:DEPTH * 18], self.gsub[:], "d4", rd, [])
            self.S.barrier()

    def norm_block(self, hb, hkey, n, sq, tmp, rt, out_fn, gsc, sh_fn, v, pk, okey, offload=False):
        hkeys = list(hkey) if isinstance(hkey, list) else [hkey]
        onesb = self.cstb[:, 2, :]
        if offload:
            self.tt("pool", sq[:, :, 0:n], hb[:, :, 0:n], hb[:, :, 0:n], ALU.mult, hkeys, [pk + "sq"])
        else:
            self.act(sq[:, :, 0:n], hb[:, :, 0:n], AF.Square, hkeys, [pk + "sq"])
        pst = self.ps[6]
        for kc in range(8):
            self.mm(pst[:, 0:n], onesb, sq[:, kc, 0:n], kc == 0, kc == 7, [pk + "sq", "cstb"], [("ps", 6)])
        self.act(rt[:, 0:n], pst[:, 0:n], AF.Sqrt, [("ps", 6)], [pk + "rt"], bias=EPS)
        self.recip(rt[:, 0:n], rt[:, 0:n], [pk + "rt"], [pk + "rt"])
        for kc in range(8):
            self.stt(tmp[:, kc, 0:n], hb[:, kc, 0:n], gsc(kc, v), rt[:, 0:n], ALU.mult, ALU.mult,
                     hkeys + [pk + "rt"], [(pk + "tmp", kc)])
            sh = sh_fn(kc, v)
            if offload:
                if sh is None:
                    self.ts("pool", out_fn(kc), tmp[:, kc, 0:n], 1.0, ALU.mult, [(pk + "tmp", kc)], [okey], s2=1.0, op1=ALU.mult)
                else:
                    self.ts("pool", out_fn(kc), tmp[:, kc, 0:n], sh, ALU.add, [(pk + "tmp", kc)], [okey], s2=1.0, op1=ALU.mult)
            elif sh is None:
                self.act(out_fn(kc), tmp[:, kc, 0:n], AF.Identity, [(pk + "tmp", kc)], [okey])
            else:
                self.act(out_fn(kc), tmp[:, kc, 0:n], AF.Identity, [(pk + "tmp", kc)], [okey], bias=sh)

    def tok_blocks(self, n):
        out = [(0, CL, 1)]
        for c0 in range(CL, T, n):
            out.append((c0, n, 0))
        return out

    def phase_proj(self, l):
        nc = self.nc
        hsrc = self.xin if l == 0 else self.hT
        with contextlib.ExitStack() as es:
            def sb(name, shape, dt):
                return es.enter_context(nc.sbuf_tensor("a%d_%s" % (l, name), list(shape), dt))
            uT = sb("uT", [128, 8, T], BF16)
            hsrc_v = hsrc.rearrange("(kc p) t -> p kc t", p=128)
            if l > 0:
                self.dma("sp", uT[:], self.u1T.rearrange("(kc p) t -> p kc t", p=128), "uT", ["u1T"], ["uT"])
            with contextlib.ExitStack() as es2:
                def sb2(name, shape, dt):
                    return es2.enter_context(nc.sbuf_tensor("a%d_%s" % (l, name), list(shape), dt))
                hb = [sb2("hb%d" % i, [128, 8, 256], F32) for i in range(2)]
                sq = sb2("sq", [128, 8, 256], BF16)
                tmp = sb2("tmp", [128, 8, 256], F32)
                rt = sb2("rt", [128, 256], F32)
                for bi, (c0, n, isctx) in enumerate(self.tok_blocks(256) if l == 0 else []):
                    s = bi % 2
                    self.dma("sp", hb[s][:], hsrc_v[:, :, c0:c0 + n], ("hb", s), ["hT"], [("hb", s)])
                    self.norm_block(hb[s], ("hb", s), n, sq, tmp, rt,
                                    lambda kc, c0=c0, n=n: uT[:, kc, c0:c0 + n],
                                    lambda kc, v: self.gsc1[:, l, kc, v:v + 1],
                                    lambda kc, v: self.mod[:, l, 0 + kc, v:v + 1], isctx, "n1", "uT")
                self.S.barrier()
            w = [sb("w%d" % i, [128, 8, 1024], BF16) for i in range(2)]
            stage = [sb("st%d" % i, [128, T], BF16) for i in range(2)]
            tcb = [sb("tcb_%d" % i, [128, 512], BF16) for i in range(3)]
            tsb = [sb("tsb_%d" % i, [128, 512], BF16) for i in range(3)]
            t16 = [sb("t16_%d" % i, [128, 512], BF16) for i in range(3)]
            ra = [sb("ra%d" % i, [128, 512], F32) for i in range(3)]
            cs = sb("cs", [128, 2, SEQ], BF16)
            for hh in range(2):
                self.dma("pool", cs[:, 0, hh * 2048:(hh + 1) * 2048], self.cos_d[:, hh * 2048:(hh + 1) * 2048], "cs", [], ["cs"])
                self.dma("pool", cs[:, 1, hh * 2048:(hh + 1) * 2048], self.sin_d[:, hh * 2048:(hh + 1) * 2048], "cs", [], ["cs"])
            bvb = sb("bvb", [128, 1024], F32)
            vst = [sb("vst%d" % i, [128, 1024], BF16) for i in range(2)]
            stages = [("RX", [(OFF_RX, 1024)]), ("K", [(OFF_K, 1024)]), ("V", [(OFF_V, 1024)]),
                      ("RG", [(OFF_RG, 1024)]), ("Q", [(OFF_Q, 1024)]),
                      ("CVG0", [(OFF_CV, 512), (OFF_CG, 512)]), ("CVG1", [(OFF_CV + 512, 512), (OFF_CG + 512, 512)]),
                      ("G0", [(OFF_G, 1024)]), ("G1", [(OFF_G + 1024, 1024)]), ("G2", [(OFF_G + 2048, 1024)])]
            self.dma("sp", bvb[:], self.bv_d[l:l + 1, :].partition_broadcast(128), "bvb", [], ["bvb"])
            blocks = self.tok_blocks(512)
            psi = 0
            sti = 0
            csi = 0
            ti = 0

            def load_w(si):
                slot = si % 2
                off = 0
                for (c0, n) in stages[si][1]:
                    src = self.w_in[l, :, c0:c0 + n].rearrange("(kc p) n -> p kc n", p=128)
                    self.dma("pool", w[slot][:, :, off:off + n], src, ("w", slot), [], [("w", slot)])
                    off += n
            load_w(0)
            dq = []
            for si, (sname, _) in enumerate(stages):
                slot = si % 2
                wk = ("w", slot)
                if si + 1 < len(stages):
                    load_w(si + 1)
                if sname == "V":
                    for tc in range(T // 128):
                        vs = tc % 2
                        for half in range(2):
                            pst = self.ps[psi % 4]
                            pk = ("ps", psi % 4)
                            psi += 1
                            for kc in range(8):
                                self.mm(pst[:, :], uT[:, kc, tc * 128:(tc + 1) * 128], w[slot][:, kc, half * 512:(half + 1) * 512],
                                        kc == 0, kc == 7, ["uT", wk], [pk])
                            self.tt("dve", vst[vs][:, half * 512:(half + 1) * 512], pst[:, :], bvb[:, half * 512:(half + 1) * 512],
                                    ALU.add, [pk, "bvb"], [("vst", vs)])
                        self.dma("sp", self.Vs[tc * 128:(tc + 1) * 128, :], vst[vs][:], ("vst", vs), [("vst", vs)], ["Vs"])
                    continue
                if sname.startswith("CVG"):
                    half = int(sname[3])
                    for jj in range(4):
                        j = half * 4 + jj
                        st = stage[sti % 2]
                        stk = ("stage", sti % 2)
                        sti += 1
                        for (c0, n, isctx) in blocks:
                            pa, pb = self.ps[psi % 4], self.ps[(psi + 1) % 4]
                            pka, pkb = ("ps", psi % 4), ("ps", (psi + 1) % 4)
                            psi += 2
                            for kc in range(8):
                                self.mm(pa[:, 0:n], w[slot][:, kc, jj * 128:(jj + 1) * 128], uT[:, kc, c0:c0 + n],
                                        kc == 0, kc == 7, ["uT", wk], [pka])
                            for kc in range(8):
                                self.mm(pb[:, 0:n], w[slot][:, kc, 512 + jj * 128:512 + (jj + 1) * 128], uT[:, kc, c0:c0 + n],
                                        kc == 0, kc == 7, ["uT", wk], [pkb])
                            r = ra[ti % 2]
                            rk = ("ra", ti % 2)
                            ti += 1
                            self.act(r[:, 0:n], pb[:, 0:n], AF.Sigmoid, [pkb, "pp"], [rk], bias=self.ppc(l, "b_in", OFF_CG // 128 + j))
                            self.stt(st[:, c0:c0 + n], pa[:, 0:n], self.ppc(l, "b_in", OFF_CV // 128 + j), r[:, 0:n],
                                     ALU.add, ALU.mult, [pka, rk, "pp"], [(stk, c0)])
                        self.dma("sp", self.zT[j * 128:(j + 1) * 128, :], st[:], stk, [(stk, b_[0]) for b_ in blocks], ["zT"])
                    continue
                off = stages[si][1][0][0]
                for j in range(8):
                    st = stage[sti % 2]
                    stk = ("stage", sti % 2)
                    sti += 1
                    bias = self.ppc(l, "b_in", off // 128 + j)
                    for (c0, n, isctx) in blocks:
                        pst = self.ps[psi % 4]
                        pk = ("ps", psi % 4)
                        psi += 1
                        for kc in range(8):
                            self.mm(pst[:, 0:n], w[slot][:, kc, j * 128:(j + 1) * 128], uT[:, kc, c0:c0 + n],
                                    kc == 0, kc == 7, ["uT", wk], [pk])
                        while len(dq) > 1:
                            dq.pop(0)()
                        if sname in ("K", "Q") and not isctx:
                            a16, tc_, ts_ = t16[ti % 3], tcb[ti % 3], tsb[ti % 3]
                            k16, kc_k, ks_k = ("t16", ti % 3), ("tcb", ti % 3), ("tsb", ti % 3)
                            ti += 1
                            q0 = c0 - CL
                            c = cs[:, :, q0:q0 + n]
                            ck = "cs"
                            self.act(a16[:, 0:n], pst[:, 0:n], AF.Identity, [pk, "pp"], [k16], bias=bias)
                            psw = self.ps[4 + (ti % 2)]
                            pswk = ("ps", 4 + (ti % 2))

                            def deferred_fn(a16=a16, tc_=tc_, ts_=ts_, k16=k16, kc_k=kc_k, ks_k=ks_k, c=c, ck=ck, psw=psw, pswk=pswk,
                                         n=n, c0=c0, st=st, stk=stk):
                                self.tt("dve", tc_[:, 0:n], a16[:, 0:n], c[:, 0, 0:n], ALU.mult, [k16, ck], [kc_k])
                                self.tt("dve", ts_[:, 0:n], a16[:, 0:n], c[:, 1, 0:n], ALU.mult, [k16, ck], [ks_k])
                                self.mm(psw[:, 0:n], self.cstb[:, 0, :], tc_[:, 0:n], True, False, [kc_k, "cstb"], [pswk])
                                self.mm(psw[:, 0:n], self.cstb[:, 1, :], ts_[:, 0:n], False, True, [ks_k, "cstb"], [pswk])
                                self.act(st[:, c0:c0 + n], psw[:, 0:n], AF.Identity, [pswk], [(stk, c0)])
                            dq.append(deferred_fn)
                        else:
                            func = {"RX": AF.Identity, "K": AF.Identity, "Q": AF.Identity, "RG": AF.Gelu_apprx_tanh}.get(sname, AF.Sigmoid)
                            self.act(st[:, c0:c0 + n], pst[:, 0:n], func, [pk, "pp"], [(stk, c0)], bias=bias)
                    while dq:
                        dq.pop(0)()
                    dst = {"RX": self.rxT, "K": self.kT, "Q": self.qT, "RG": self.rgT}.get(sname)
                    if dst is None:
                        gi = int(sname[1])
                        dst_ap = self.gT[(gi * 8 + j) * 128:(gi * 8 + j + 1) * 128, :]
                        dk = "gT"
                    else:
                        dst_ap = dst[j * 128:(j + 1) * 128, :]
                        dk = sname + "T"
                    self.dma("sp", dst_ap, st[:], stk, [(stk, b_[0]) for b_ in blocks], [dk])

    def phase_rglru(self, l):
        nc = self.nc
        WP = T + 6
        WO = T + 3
        with contextlib.ExitStack() as es:
            def sb(name, shape, dt):
                return es.enter_context(nc.sbuf_tensor("b%d_%s" % (l, name), list(shape), dt))
            rxp = [sb("rxp%d" % i, [128, WP], BF16) for i in range(2)]
            gel = [sb("gel%d" % i, [128, T], BF16) for i in range(2)]
            tro = [sb("tro%d" % i, [128, T], BF16) for i in range(2)]
            xr = sb("xr", [128, WO], F32)
            xrb = sb("xrb", [128, WO], BF16)
            rb = sb("r", [128, WO], F32)
            ib = sb("i", [128, WO], F32)
            tb = sb("t", [128, WO], F32)
            hf = sb("hf", [128, WO], F32)
            bd = sb("bd", [128, 8, 4, 128], BF16)
            dg = [sb("dg%d" % i, [128, 4, 128], BF16) for i in range(2)]
            for i in range(2):
                self.memset("pool", rxp[i][:], 0.0, [("rxp", i)])
            self.memset("pool", bd[:], 0.0, ["bd"])
            for d in range(2):
                for g, wsrc in enumerate((self.rnn_w_a, self.rnn_w_x)):
                    for half in range(2):
                        src = wsrc[l, d].rearrange("(j h) i o -> h i j o", h=2)[half]
                        self.dma("pool", bd[half * 64:(half + 1) * 64, :, d * 2 + g, half * 64:(half + 1) * 64], src,
                                 "bd", [], ["bd"])
            identb = self.cstb[:, 0, :]
            oblocks = [(c0, min(512, WO - c0)) for c0 in range(0, WO, 512)]
            psi = 0
            def load_chunk(j):
                s = j % 2
                self.dma("sp", rxp[s][:, 2:2 + CL], self.rxT[j * 128:(j + 1) * 128, 0:CL], ("rxp", s), ["rxT"], [("rxp", s)])
                self.dma("sp", rxp[s][:, 261:261 + SEQ], self.rxT[j * 128:(j + 1) * 128, CL:T], ("rxp", s), ["rxT"], [("rxp", s)])
                self.dma("sp", gel[s][:], self.rgT[j * 128:(j + 1) * 128, :], ("gel", s), ["rgT"], [("gel", s)])
            load_chunk(0)
            for j in range(8):
                s = j % 2
                if j + 1 < 8:
                    load_chunk(j + 1)
                for k in range(4):
                    self.ts("dve", dg[s][:, k, :], identb, self.ppc(l, "rcw", j * 4 + k), ALU.mult, ["cstb", "pp"], [("dg", s)])
                for (c0, n) in oblocks:
                    pst = self.ps[psi % 4]
                    pk = ("ps", psi % 4)
                    psi += 1
                    for k in range(4):
                        self.mm(pst[:, 0:n], dg[s][:, k, :], rxp[s][:, c0 + k:c0 + k + n], k == 0, k == 3,
                                [("dg", s), ("rxp", s)], [pk])
                    self.act(xr[:, c0:c0 + n], pst[:, 0:n], AF.Identity, [pk, "pp"], ["xr"], bias=self.ppc(l, "rcb", j))
                self.copy("dve", xrb[:], xr[:], ["xr"], ["xrb"])
                for d in range(2):
                    for g, (dst, bname, dk) in enumerate(((rb, "rba", "r"), (ib, "rbx", "i"))):
                        for (c0, n) in oblocks:
                            pst = self.ps[psi % 4]
                            pk = ("ps", psi % 4)
                            psi += 1
                            self.mm(pst[:, 0:n], bd[:, j, d * 2 + g, :], xrb[:, c0:c0 + n], True, True, ["bd", "xrb"], [pk])
                            self.act(dst[:, c0:c0 + n], pst[:, 0:n], AF.Sigmoid, [pk, "pp"], [dk],
                                     bias=self.ppc(l, bname, d * 8 + j))
                    self.act(rb[:], rb[:], AF.Exp, ["r", ("cdec", l)], ["r"], scale=self.cdec[:, l, d * 8 + j:d * 8 + j + 1])
                    self.tt("dve", tb[:], rb[:], rb[:], ALU.mult, ["r"], ["t"])
                    self.act(tb[:], tb[:], AF.Sqrt, ["t"], ["t"], bias=1.0, scale=-1.0)
                    self.tt("pool", ib[:], ib[:], xr[:], ALU.mult, ["i", "xr"], ["i"])
                    self.tt("dve", ib[:], ib[:], tb[:], ALU.mult, ["i", "t"], ["i"])
                    dst = hf if d == 0 else tb
                    dk = "hf" if d == 0 else "t"
                    if d == 0:
                        self.S.op("dve", lambda e, dst=dst: e.tensor_tensor_scan(
                            out=dst[:, 0:CL], data0=rb[:, 0:CL], data1=ib[:, 0:CL], initial=0.0, op0=ALU.mult, op1=ALU.add),
                            ["r", "i"], [dk])
                        self.S.op("dve", lambda e, dst=dst: e.tensor_tensor_scan(
                            out=dst[:, 259:WO], data0=rb[:, 259:WO], data1=ib[:, 259:WO], initial=dst[:, CL - 1:CL],
                            op0=ALU.mult, op1=ALU.add), ["r", "i", dk], [dk])
                    else:
                        self.S.op("dve", lambda e, dst=dst: e.tensor_tensor_scan(
                            out=dst[:, 0:CL][:, ::-1], data0=rb[:, 0:CL][:, ::-1], data1=ib[:, 0:CL][:, ::-1], initial=0.0,
                            op0=ALU.mult, op1=ALU.add), ["r", "i", "t"], [dk])
                        self.S.op("dve", lambda e, dst=dst: e.tensor_tensor_scan(
                            out=dst[:, 259:WO][:, ::-1], data0=rb[:, 259:WO][:, ::-1], data1=ib[:, 259:WO][:, ::-1],
                            initial=dst[:, 0:1], op0=ALU.mult, op1=ALU.add), ["r", "i", dk], [dk])
                self.tt("pool", hf[:], hf[:], tb[:], ALU.add, ["hf", "t"], ["hf"])
                self.tt("dve", tro[s][:, 0:CL], hf[:, 0:CL], gel[s][:, 0:CL], ALU.mult, ["hf", ("gel", s)], [("tro", s)])
                self.tt("dve", tro[s][:, CL:T], hf[:, 259:WO], gel[s][:, CL:T], ALU.mult, ["hf", ("gel", s)], [("tro", s)])
                self.dma("sp", self.trT[j * 128:(j + 1) * 128, :], tro[s][:], ("tro", s), [("tro", s)], ["trT"])

    def phase_conv(self, l):
        nc = self.nc
        WP = 15 + CL + 30 + SEQ + 15
        with contextlib.ExitStack() as es:
            def sb(name, shape, dt):
                return es.enter_context(nc.sbuf_tensor("c%d_%s" % (l, name), list(shape), dt))
            zp = [sb("zp%d" % i, [128, WP], BF16) for i in range(2)]
            zo = [sb("zo%d" % i, [128, T], BF16) for i in range(2)]
            dg = [sb("dg%d" % i, [128, 31, 128], BF16) for i in range(2)]
            for i in range(2):
                self.memset("pool", zp[i][:], 0.0, [("zp", i)])
            identb = self.cstb[:, 0, :]
            blocks = [(0, CL, 0)] + [(286 + 512 * i, 512, CL + 512 * i) for i in range(8)]
            psi = 0
            def load_chunk(j):
                s = j % 2
                self.dma("sp", zp[s][:, 15:15 + CL], self.zT[j * 128:(j + 1) * 128, 0:CL], ("zp", s), ["zT"], [("zp", s)])
                self.dma("sp", zp[s][:, 301:301 + SEQ], self.zT[j * 128:(j + 1) * 128, CL:T], ("zp", s), ["zT"], [("zp", s)])
            load_chunk(0)
            for j in range(8):
                s = j % 2
                if j + 1 < 8:
                    load_chunk(j + 1)
                for k in range(31):
                    self.ts("dve" if k % 2 == 0 else "pool", dg[s][:, k, :], identb, self.ppc(l, "cdw", j * 31 + k), ALU.mult,
                            ["cstb", "pp"], [("dg", s, k)])
                for (c0, n, o0) in blocks:
                    pst = self.ps[psi % 4]
                    pk = ("ps", psi % 4)
                    psi += 1
                    for k in range(31):
                        self.mm(pst[:, 0:n], dg[s][:, k, :], zp[s][:, c0 + k:c0 + k + n], k == 0, k == 30,
                                [("dg", s, k), ("zp", s)], [pk])
                    self.act(zo[s][:, o0:o0 + n], pst[:, 0:n], AF.Identity, [pk, "pp"], [("zo", s)], bias=self.ppc(l, "cdb", j))
                self.dma("sp", self.zcT[j * 128:(j + 1) * 128, :], zo[s][:], ("zo", s), [("zo", s)], ["zcT"])

    def phase_rc(self, l):
        nc = self.nc
        WP = T + 6
        WO = T + 3
        WZ = 15 + CL + 30 + SEQ + 15
        with contextlib.ExitStack() as es:
            def sb(name, shape, dt):
                return es.enter_context(nc.sbuf_tensor("r%d_%s" % (l, name), list(shape), dt))
            rxp = [sb("rxp%d" % i, [128, WP], BF16) for i in range(2)]
            gel = [sb("gel%d" % i, [128, T], BF16) for i in range(2)]
            tro = [sb("tro0", [128, T], BF16)] * 2
            xr = sb("xr", [128, WO], F32)
            xrb = sb("xrb", [128, WO], BF16)
            rb = sb("r", [128, WO], F32)
            ib = sb("i", [128, WO], F32)
            tb = sb("t", [128, WO], F32)
            hf = sb("hf", [128, WO], F32)
            bd = sb("bd", [128, 8, 4, 128], BF16)
            dg4 = [sb("dg4_%d" % i, [128, 4, 128], BF16) for i in range(2)]
            zp = [sb("zp%d" % i, [128, WZ], BF16) for i in range(2)]
            zo = sb("zo", [128, T], BF16)
            dg31 = [sb("dg31_%d" % i, [128, 31, 128], BF16) for i in range(2)]
            for i in range(2):
                self.memset("pool", rxp[i][:], 0.0, [("rxp", i)])
                self.memset("pool", zp[i][:], 0.0, [("zp", i)])
            self.memset("pool", bd[:], 0.0, ["bd"])
            for d in range(2):
                for g, wsrc in enumerate((self.rnn_w_a, self.rnn_w_x)):
                    for half in range(2):
                        src = wsrc[l, d].rearrange("(j h) i o -> h i j o", h=2)[half]
                        self.dma("pool", bd[half * 64:(half + 1) * 64, :, d * 2 + g, half * 64:(half + 1) * 64], src,
                                 "bd", [], ["bd"])
            identb = self.cstb[:, 0, :]
            oblocks = [(c0, min(512, WO - c0)) for c0 in range(0, WO, 512)]
            cblocks = [(0, CL, 0)] + [(286 + 512 * i, 512, CL + 512 * i) for i in range(8)]
            cnt = {"c": 0, "g": 0}

            def load_chunk(j):
                s = j % 2
                self.dma("sp", rxp[s][:, 2:2 + CL], self.rxT[j * 128:(j + 1) * 128, 0:CL], ("rxp", s), ["rxT"], [("rxp", s)])
                self.dma("sp", rxp[s][:, 261:261 + SEQ], self.rxT[j * 128:(j + 1) * 128, CL:T], ("rxp", s), ["rxT"], [("rxp", s)])
                self.dma("sp", gel[s][:], self.rgT[j * 128:(j + 1) * 128, :], ("gel", s), ["rgT"], [("gel", s)])
                self.dma("sp", zp[s][:, 15:15 + CL], self.zT[j * 128:(j + 1) * 128, 0:CL], ("zp", s), ["zT"], [("zp", s)])
                self.dma("sp", zp[s][:, 301:301 + SEQ], self.zT[j * 128:(j + 1) * 128, CL:T], ("zp", s), ["zT"], [("zp", s)])

            def conv31_blocks(j, s, b0, b1):
                for bi in range(b0, b1):
                    c0, n, o0 = cblocks[bi]
                    b = cnt["c"] % 4
                    cnt["c"] += 1
                    pst, pk = self.ps[b], ("ps", b)
                    for k in range(31):
                        self.mm(pst[:, 0:n], dg31[s][:, k, :], zp[s][:, c0 + k:c0 + k + n], k == 0, k == 30,
                                [("dg31", s), ("zp", s)], [pk])
                    self.act(zo[:, o0:o0 + n], pst[:, 0:n], AF.Identity, [pk, "pp"], [("zo", bi)], bias=self.ppc(l, "cdb", j))

            def gates(j, d):
                for g, (dst, bname, dk) in enumerate(((rb, "rba", "r"), (ib, "rbx", "i"))):
                    for (c0, n) in oblocks:
                        b = 4 + cnt["g"] % 3
                        cnt["g"] += 1
                        pst, pk = self.ps[b], ("ps", b)
                        self.mm(pst[:, 0:n], bd[:, j, d * 2 + g, :], xrb[:, c0:c0 + n], True, True, ["bd", "xrb"], [pk])
                        self.act(dst[:, c0:c0 + n], pst[:, 0:n], AF.Sigmoid, [pk, "pp"], [dk], bias=self.ppc(l, bname, d * 8 + j))

            def scan(d):
                dst = hf if d == 0 else tb
                dk = "hf" if d == 0 else "t"
                if d == 0:
                    self.S.op("dve", lambda e: e.tensor_tensor_scan(
                        out=dst[:, 0:CL], data0=rb[:, 0:CL], data1=ib[:, 0:CL], initial=0.0, op0=ALU.mult, op1=ALU.add),
                        ["r", "i"], [dk])
                    self.S.op("dve", lambda e: e.tensor_tensor_scan(
                        out=dst[:, 259:WO], data0=rb[:, 259:WO], data1=ib[:, 259:WO], initial=dst[:, CL - 1:CL],
                        op0=ALU.mult, op1=ALU.add), ["r", "i", dk], [dk])
                else:
                    self.S.op("dve", lambda e: e.tensor_tensor_scan(
                        out=dst[:, 0:CL][:, ::-1], data0=rb[:, 0:CL][:, ::-1], data1=ib[:, 0:CL][:, ::-1], initial=0.0,
                        op0=ALU.mult, op1=ALU.add), ["r", "i", "t"], [dk])
                    self.S.op("dve", lambda e: e.tensor_tensor_scan(
                        out=dst[:, 259:WO][:, ::-1], data0=rb[:, 259:WO][:, ::-1], data1=ib[:, 259:WO][:, ::-1],
                        initial=dst[:, 0:1], op0=ALU.mult, op1=ALU.add), ["r", "i", dk], [dk])

            def build_dg31(j):
                for k in range(31):
                    self.ts("pool", dg31[j % 2][:, k, :], identb, self.ppc(l, "cdw", j * 31 + k), ALU.mult, ["cstb", "pp"],
                            [("dg31", j % 2)], s2=1.0, op1=ALU.mult)

            load_chunk(0)
            for j in range(8):
                s = j % 2
                if j + 1 < 8:
                    load_chunk(j + 1)
                for k in range(4):
                    self.ts("dve", dg4[s][:, k, :], identb, self.ppc(l, "rcw", j * 4 + k), ALU.mult, ["cstb", "pp"], [("dg4", s)])
                if j == 0:
                    build_dg31(0)
                if j + 1 < 8:
                    build_dg31(j + 1)
                for (c0, n) in oblocks:
                    b = 4 + cnt["g"] % 3
                    cnt["g"] += 1
                    pst, pk = self.ps[b], ("ps", b)
                    for k in range(4):
                        self.mm(pst[:, 0:n], dg4[s][:, k, :], rxp[s][:, c0 + k:c0 + k + n], k == 0, k == 3,
                                [("dg4", s), ("rxp", s)], [pk])
                    self.act(xr[:, c0:c0 + n], pst[:, 0:n], AF.Identity, [pk, "pp"], ["xr"], bias=self.ppc(l, "rcb", j))
                self.copy("dve", xrb[:], xr[:], ["xr"], ["xrb"])
                conv31_blocks(j, s, 0, 3)
                gates(j, 0)
                self.act(rb[:], rb[:], AF.Exp, ["r", ("cdec", l)], ["r"], scale=self.cdec[:, l, j:j + 1])
                self.tt("dve", tb[:], rb[:], rb[:], ALU.mult, ["r"], ["t"])
                self.tt("pool", ib[:], ib[:], xr[:], ALU.mult, ["i", "xr"], ["i"])
                conv31_blocks(j, s, 3, 5)
                self.act(tb[:], tb[:], AF.Sqrt, ["t"], ["t"], bias=1.0, scale=-1.0)
                self.tt("dve", ib[:], ib[:], tb[:], ALU.mult, ["i", "t"], ["i"])
                scan(0)
                conv31_blocks(j, s, 5, 9)
                self.dma("sp", self.zcT[j * 128:(j + 1) * 128, :], zo[:], "zo", [("zo", bi) for bi in range(9)],
                         [("zo", bi) for bi in range(9)] + ["zcT"])
                gates(j, 1)
                self.act(rb[:], rb[:], AF.Exp, ["r", ("cdec", l)], ["r"], scale=self.cdec[:, l, 8 + j:8 + j + 1])
                self.tt("dve", tb[:], rb[:], rb[:], ALU.mult, ["r"], ["t"])
                self.tt("pool", ib[:], ib[:], xr[:], ALU.mult, ["i", "xr"], ["i"])
                self.act(tb[:], tb[:], AF.Sqrt, ["t"], ["t"], bias=1.0, scale=-1.0)
                self.tt("dve", ib[:], ib[:], tb[:], ALU.mult, ["i", "t"], ["i"])
                scan(1)
                self.tt("pool", hf[:], hf[:], tb[:], ALU.add, ["hf", "t"], ["hf"])
                self.tt("dve", tro[s][:, 0:CL], hf[:, 0:CL], gel[s][:, 0:CL], ALU.mult, ["hf", ("gel", s)], ["tro"])
                self.tt("dve", tro[s][:, CL:T], hf[:, 259:WO], gel[s][:, CL:T], ALU.mult, ["hf", ("gel", s)], ["tro"])
                self.dma("sp", self.trT[j * 128:(j + 1) * 128, :], tro[s][:], "tro", ["tro"], ["trT"])

    def phase_attn(self, l):
        nc = self.nc
        ctx_out = l < DEPTH - 1
        NKC = T // 128
        with contextlib.ExitStack() as es:
            def sb(name, shape, dt):
                return es.enter_context(nc.sbuf_tensor("d%d_%s" % (l, name), list(shape), dt))
            kh = [sb("kh%d" % i, [128, T], BF16) for i in range(2)]
            qh = [sb("qh%d" % i, [128, 2, T], BF16) for i in range(2)]
            vh = [sb("vh%d" % i, [128, NKC, 132], BF16) for i in range(2)]
            oh = [sb("oh%d" % i, [128, T], BF16) for i in range(2)]
            pT = [sb("pT%d" % i, [128, 2, 256], BF16) for i in range(4)]
            acs = [sb("acs%d" % i, [128, 2, 132], F32) for i in range(4)]
            o1 = [sb("o1_%d" % i, [128, 128], F32) for i in range(4)]
            o2 = [sb("o2_%d" % i, [128, 128], F32) for i in range(4)]
            on = [sb("on_%d" % i, [128, 128], BF16) for i in range(4)]
            junk = sb("junk", [128, 128], F32)
            sm = [sb("sm%d" % i, [128, 8], F32) for i in range(4)]
            for i in range(2):
                self.memset("pool", vh[i][:, :, 128:129], 1.0, [("vh", i)])
                self.memset("pool", qh[i][:], 0.0, [("qh", i)])
            identb = self.cstb[:, 0, :]
            st = {"sci": 0, "pti": 0, "ei": 0}
            pending = []
            for i, wsrc in enumerate((self.w_rnn_o, self.w_attn_o, self.w_conv_o, self.w_out)):
                self.dma("pool", self.wm_bf[i], wsrc[l], "precast", [], ["wm_bf"])
            for q in range(4):
                self.dma("pool", self.w1_bf[:, q * 1408:(q + 1) * 1408], self.w_ffn_in[l, :, q * 1408:(q + 1) * 1408], "precast", [], ["w1_bf"])
            for q in range(2):
                self.dma("pool", self.w2_bf[q * 1408:(q + 1) * 1408, :], self.w_ffn_out[l, q * 1408:(q + 1) * 1408, :], "precast", [], ["w2_bf"])

            def run_pending(step):
                keep = []
                for trig, fn in pending:
                    if step is None or step >= trig:
                        fn()
                    else:
                        keep.append((trig, fn))
                pending[:] = keep

            def load_head(h):
                s = h % 2
                kk, qk, vk = ("kh", s), ("qh", s), ("vh", s)
                self.dma("sp", kh[s][:], self.kT[h * 128:(h + 1) * 128, :], kk, ["KT"], [kk])
                self.dma("sp", qh[s][0:64, 0, :], self.qT[h * 128:h * 128 + 64, :], qk, ["QT"], [qk])
                self.dma("sp", qh[s][64:128, 1, :], self.qT[h * 128 + 64:(h + 1) * 128, :], qk, ["QT"], [qk])
                vsrc = self.Vs[:, h * 128:(h + 1) * 128].rearrange("(kc p) e -> p kc e", p=128)
                for k0 in range(0, NKC, 6):
                    k1 = min(NKC, k0 + 6)
                    self.dma("sp", vh[s][:, k0:k1, 0:128], vsrc[:, k0:k1, :], vk, ["Vs"], [vk])

            def make_stages(e, qs, q0, acc, acck, s, ok):
                smk = ("sm", e)

                def stage_a():
                    for m in range(2):
                        self.copy("dve", acs[e][:, m, 0:129], acc[m][qs][:, 0:129], [acck[m][qs]], [("acs", e, m)])

                def stage_b1():
                    self.recip(sm[e][:, 0:1], acs[e][:, 0, 128:129], [("acs", e, 0)], [(smk, 0)])
                    self.recip(sm[e][:, 1:2], acs[e][:, 1, 128:129], [("acs", e, 1)], [(smk, 1)])
                    self.ts("dve", sm[e][:, 2:3], sm[e][:, 1:2], self.nlam[:, l:l + 1], ALU.mult, [(smk, 1), ("nlam", l)], [(smk, 2)])
                    self.ts("dve", o1[e][:], acs[e][:, 0, 0:128], sm[e][:, 0:1], ALU.mult, [("acs", e, 0), (smk, 0)], [("o1", e)])
                    self.stt(o2[e][:], acs[e][:, 1, 0:128], sm[e][:, 2:3], o1[e][:], ALU.mult, ALU.add,
                             [("acs", e, 1), (smk, 2), ("o1", e)], [("o2", e)])
                    self.S.op("dve", lambda en: en.scalar_tensor_tensor(out=junk[:], in0=o2[e][:], scalar=1.0, in1=o2[e][:],
                                                                        op0=ALU.mult, op1=ALU.mult, accum_out=sm[e][:, 3:4]),
                              [("o2", e)], ["junk", (smk, 3)])

                def stage_b2():
                    self.act(sm[e][:, 4:5], sm[e][:, 3:4], AF.Ln, [(smk, 3)], [(smk, 4)], bias=EPS, scale=1.0 / 128.0)
                    self.act(sm[e][:, 5:6], sm[e][:, 4:5], AF.Exp, [(smk, 4)], [(smk, 5)], scale=-0.5)
                    self.ts("dve", on[e][:], o2[e][:], sm[e][:, 5:6], ALU.mult, [("o2", e), (smk, 5)], [("on", e)])

                def stage_b3():
                    self.S.op("pe", lambda en: en.transpose(out=self.psb[:, e * 128:(e + 1) * 128], in_=on[e][:], identity=identb),
                              [("on", e), "cstb"], [("psb", e)])
                    self.ts("dve", oh[s][:, q0 + qs * 128:q0 + (qs + 1) * 128], self.psb[:, e * 128:(e + 1) * 128],
                            self.gsub[:, l:l + 1], ALU.mult, [("psb", e), ("gsub", l)], [(ok, q0, qs)])
                return stage_a, stage_b1, stage_b2, stage_b3

            load_head(0)
            for h in range(8):
                s = h % 2
                kk, qk, vk, ok = ("kh", s), ("qh", s), ("vh", s), ("oh", s)
                if h + 1 < 8:
                    load_head(h + 1)
                qblocks = []
                if ctx_out:
                    qblocks.append((0, 2))
                for c0 in range(CL, T, 256):
                    qblocks.append((c0, NKC))
                okeys = []
                acc = [[self.ps[3 + m * 2 + qs] for qs in range(2)] for m in range(2)]
                acck = [[("ps", 3 + m * 2 + qs) for qs in range(2)] for m in range(2)]

                def score(q0, kc):
                    b = st["sci"] % 3
                    st["sci"] += 1
                    self.mm(self.ps[b][:, :].rearrange("p (m q) -> p m q", m=2), kh[s][:, kc * 128:(kc + 1) * 128],
                            qh[s][:, :, q0:q0 + 256], True, True, [kk, qk], [("ps", b)])
                    return b

                def pv(q0, nk, kc, b):
                    pi = st["pti"] % 4
                    st["pti"] += 1
                    self.act(pT[pi][:].rearrange("p m q -> p (m q)"), self.ps[b][:, :], AF.Exp, [("ps", b)], [("pT", pi)],
                             scale=0.125)
                    for m in range(2):
                        for qs in range(2):
                            self.mm(acc[m][qs][:, 0:129], pT[pi][:, m, qs * 128:(qs + 1) * 128], vh[s][:, kc, 0:129],
                                    kc == 0, kc == nk - 1, [("pT", pi), vk], [acck[m][qs]])
                    run_pending(kc)
                    if kc == nk - 1:
                        run_pending(None)
                        for qs in range(2):
                            e = st["ei"] % 4
                            st["ei"] += 1
                            a_, b1_, b2_, b3_ = make_stages(e, qs, q0, acc, acck, s, ok)
                            a_()
                            pending.append((2, b1_))
                            pending.append((9, b2_))
                            pending.append((18, b3_))
                            okeys.append((ok, q0, qs))
                steps = [(q0, nk, kc) for (q0, nk) in qblocks for kc in range(nk)]
                bq = []
                for (q0, nk, kc) in steps:
                    bq.append((q0, nk, kc, score(q0, kc)))
                    if len(bq) > 2:
                        pv(*bq.pop(0))
                while bq:
                    pv(*bq.pop(0))
                run_pending(None)
                if ctx_out:
                    self.dma("sp", self.oT[h * 128:(h + 1) * 128, :], oh[s][:], ok, okeys, ["oT"])
                else:
                    self.dma("sp", self.oT[h * 128:(h + 1) * 128, CL:T], oh[s][:, CL:T], ok, okeys, ["oT"])

    def load_sq_w(self, dst, src, key):
        self.dma("pool", dst[:], src.rearrange("(kc p) n -> p kc n", p=128), key, [], [key])

    def phase_merge(self, l):
        nc = self.nc
        last = l == DEPTH - 1
        NB = 256
        with contextlib.ExitStack() as es:
            def sb(name, shape, dt):
                return es.enter_context(nc.sbuf_tensor("e%d_%s" % (l, name), list(shape), dt))
            wr = sb("wr", [128, 8, D], BF16)
            wa = sb("wa", [128, 8, D], BF16)
            wc = sb("wc", [128, 8, D], BF16)
            wo = sb("wo", [128, 8, D], BF16)
            tr = [sb("tr%d" % i, [128, 8, NB], BF16) for i in range(2)]
            ot = [sb("ot%d" % i, [128, 8, NB], BF16) for i in range(2)]
            zc = [sb("zc%d" % i, [128, 8, 2, NB], BF16) for i in range(3)]
            gg = [sb("gg%d" % i, [128, 24, NB], BF16) for i in range(2)]
            hb = [sb("hb%d" % i, [128, 8, NB], F32) for i in range(2)]
            zl = [sb("zl%d" % i, [128, 8, NB], BF16) for i in range(2)]
            mu = sb("mu", [128, NB], F32)
            var = sb("var", [128, NB], F32)
            t1 = [sb("t1_%d" % i, [128, NB], F32) for i in range(2)]
            t2 = [sb("t2_%d" % i, [128, NB], F32) for i in range(2)]
            m1 = [sb("m1_%d" % i, [128, NB], F32) for i in range(2)]
            m2 = [sb("m2_%d" % i, [128, NB], F32) for i in range(2)]
            m3 = [sb("m3_%d" % i, [128, NB], F32) for i in range(2)]
            mm_ = sb("mm", [128, 8, NB], BF16)
            sq2 = sb("sq2", [128, 8, NB], BF16)
            tmp2 = sb("tmp2", [128, 8, NB], F32)
            rt2 = sb("rt2", [128, NB], F32)
            u2o = [sb("u2o%d" % i, [128, 8, NB], BF16) for i in range(2)]
            u2v = self.u2T.rearrange("(kc p) t -> p kc t", p=128)
            for i, (dst, key) in enumerate(((wr, "wr"), (wa, "wa"), (wc, "wc"), (wo, "wo"))):
                self.dma("sp", dst[:], self.wm_bf[i].rearrange("(kc p) n -> p kc n", p=128), key, ["wm_bf"], [key])
            onesb = self.cstb[:, 2, :]
            blocks = self.tok_blocks(NB)
            if last:
                blocks = blocks[1:]
            psi = 0
            mi = 0

            def view(d, c0, n):
                return d.rearrange("(kc p) t -> p kc t", p=128)[:, :, c0:c0 + n]
            def load_a(bi):
                c0, n, v = blocks[bi]
                z = bi % 3
                self.dma("sp", zc[z][:, :, 0, :], view(self.zcT, c0, n), ("zc", z), ["zcT"], [("zc", z)])

            def prep(bi):
                c0, n, v = blocks[bi]
                s = bi % 2
                z = bi % 3
                self.dma("sp", tr[s][:], view(self.trT, c0, n), ("tr", s), ["trT"], [("tr", s)])
                self.dma("sp", ot[s][:], view(self.oT, c0, n), ("ot", s), ["oT"], [("ot", s)])
                self.dma("sp", gg[s][:], view(self.gT, c0, n), ("gg", s), ["gT"], [("gg", s)])
                self.dma("sp", hb[s][:], view(self.xin if l == 0 else self.hT, c0, n), ("hb", s),
                         ["hT"] + [("hb", s, oc) for oc in range(8)], [("hb", s, "ld")])
                self.act(zc[z][:, :, 1, :], zc[z][:, :, 0, :], AF.Square, [("zc", z)], [("zq", z)])
                pmq = self.ps[6]
                for kc in range(8):
                    self.mm(pmq[:, :].rearrange("p (a n) -> p a n", a=2), onesb, zc[z][:, kc, :, :], kc == 0, kc == 7,
                            [("zc", z), ("zq", z), "cstb"], [("ps", 6)])
                self.copy("act", mu[:], pmq[:, 0:NB], [("ps", 6)], ["mu"])
                self.tt("dve", var[:], mu[:], mu[:], ALU.mult, ["mu"], ["var"])
                self.tt("dve", var[:], pmq[:, NB:2 * NB], var[:], ALU.subtract, [("ps", 6), "var"], ["var"])
                self.ts("dve", var[:], var[:], 0.0, ALU.max, ["var"], ["var"])
                self.act(var[:], var[:], AF.Sqrt, ["var"], ["var"], bias=EPS)
                self.recip(var[:], var[:], ["var"], ["var"])
                for kc in range(8):
                    self.tt("dve", t1[kc % 2][:], zc[z][:, kc, 0, :], mu[:], ALU.subtract, [("zc", z), "mu"], [("t1", kc % 2)])
                    self.tt("pool", t2[kc % 2][:], t1[kc % 2][:], var[:], ALU.mult, [("t1", kc % 2), "var"], [("t2", kc % 2)])
                    self.act(zl[s][:, kc, :], t2[kc % 2][:], AF.Silu, [("t2", kc % 2), "pp"], [("zl", s, kc)], bias=self.ppc(l, "clb", kc),
                             scale=self.ppc(l, "clg", kc))
            load_a(0)
            if len(blocks) > 1:
                load_a(1)
            prep(0)
            for bi, (c0, n, v) in enumerate(blocks):
                s = bi % 2
                if bi + 2 < len(blocks):
                    load_a(bi + 2)
                if bi + 1 < len(blocks):
                    prep(bi + 1)
                for oc in range(8):
                    b0 = 3 * (oc % 2)
                    pr, pa, pc = self.ps[b0], self.ps[b0 + 1], self.ps[b0 + 2]
                    kr, ka, kc_ = ("ps", b0), ("ps", b0 + 1), ("ps", b0 + 2)
                    for kc in range(8):
                        self.mm(pr[:, 0:n], wr[:, kc, oc * 128:(oc + 1) * 128], tr[s][:, kc, :], kc == 0, kc == 7, ["wr", ("tr", s)], [kr])
                    for kc in range(8):
                        self.mm(pa[:, 0:n], wa[:, kc, oc * 128:(oc + 1) * 128], ot[s][:, kc, :], kc == 0, kc == 7, ["wa", ("ot", s)], [ka])
                    for kc in range(8):
                        self.mm(pc[:, 0:n], wc[:, kc, oc * 128:(oc + 1) * 128], zl[s][:, kc, :], kc == 0, kc == 7,
                                ["wc"] + [("zl", s, k2) for k2 in range(8)], [kc_])
                    e = mi % 2
                    mi += 1
                    self.tt("dve", m1[e][:], pr[:, 0:n], gg[s][:, oc, :], ALU.mult, [kr, ("gg", s)], [("m1", e)])
                    self.tt("dve", m2[e][:], pa[:, 0:n], gg[s][:, 8 + oc, :], ALU.mult, [ka, ("gg", s)], [("m2", e)])
                    self.tt("dve", m3[e][:], pc[:, 0:n], gg[s][:, 16 + oc, :], ALU.mult, [kc_, ("gg", s)], [("m3", e)])
                    self.tt("pool", m1[e][:], m1[e][:], m2[e][:], ALU.add, [("m1", e), ("m2", e)], [("m1", e)])
                    self.tt("pool", mm_[:, oc, :], m1[e][:], m3[e][:], ALU.add, [("m1", e), ("m3", e)], [("mm", oc)])
                for oc in range(8):
                    po = self.ps[oc % 6]
                    pok = ("ps", oc % 6)
                    for kc in range(8):
                        self.mm(po[:, 0:n], wo[:, kc, oc * 128:(oc + 1) * 128], mm_[:, kc, :], kc == 0, kc == 7,
                                ["wo"] + [("mm", k2) for k2 in range(8)], [pok])
                    self.stt(hb[s][:, oc, :], po[:, 0:n], self.mod[:, l, 16 + oc, v:v + 1], hb[s][:, oc, :], ALU.mult, ALU.add,
                             [pok, ("hb", s, "ld"), ("mod", l)], [("hb", s, oc)])
                ui = bi % 2
                self.norm_block(hb[s], [("hb", s, oc) for oc in range(8)], n, sq2, tmp2, rt2,
                                lambda kc, ui=ui, n=n: u2o[ui][:, kc, 0:n],
                                lambda kc, v: self.gsc2[:, l, kc, v:v + 1],
                                lambda kc, v: self.mod[:, l, 24 + kc, v:v + 1], v, "n2m", ("u2o", ui), offload=True)
                self.dma("sp", u2v[:, :, c0:c0 + n], u2o[ui][:, :, 0:n], ("u2o", ui), [("u2o", ui)], [("u2o", ui), "u2T"])
                self.dma("sp", view(self.hT, c0, n), hb[s][:], ("hb", s), [("hb", s, oc) for oc in range(8)], [("hb", s, "ld")])

    def phase_ffn(self, l):
        nc = self.nc
        last = l == DEPTH - 1
        NB = 512
        HF = DFF // 2
        NH = HF // 128
        hv = self.hT.rearrange("(kc p) t -> p kc t", p=128)
        uv = self.u2T.rearrange("(kc p) t -> p kc t", p=128)
        for half in range(2):
            if half:
                self.S.barrier()
            with contextlib.ExitStack() as es:
                def sb(name, shape, dt):
                    return es.enter_context(nc.sbuf_tensor("f%d%d_%s" % (l, half, name), list(shape), dt))
                w1a = sb("w1a", [128, 8, HF], BF16)
                w1b = sb("w1b", [128, 8, HF], BF16)
                w2 = sb("w2", [128, NH, D], BF16)
                hb3 = [sb("hb%d" % i, [128, 8, NB], F32) for i in range(3)]
                u2 = [sb("u2%d" % i, [128, 8, NB], BF16) for i in range(2)]
                si_ = [sb("si%d" % i, [128, NB], F32) for i in range(2)]
                ac = sb("ac", [128, NH, NB], BF16)
                if half == 1:
                    sq = sb("sq", [128, 8, NB], BF16)
                    tmp = sb("tmp", [128, 8, NB], F32)
                    rt = sb("rt", [128, NB], F32)
                    if last:
                        un = [sb("ob", [128, 8, NB], F32)]
                    else:
                        un = [sb("un%d" % i, [128, 8, NB], BF16) for i in range(2)]
                    u1v = self.u1T.rearrange("(kc p) t -> p kc t", p=128)
                    ov = self.out_d.rearrange("(kc p) t -> p kc t", p=128)
                self.dma("sp", w1a[:], self.w1_bf[:, half * HF:(half + 1) * HF].rearrange("(kc p) n -> p kc n", p=128),
                         "w1a", ["w1_bf"], ["w1a"])
                self.dma("sp", w1b[:], self.w1_bf[:, DFF + half * HF:DFF + (half + 1) * HF].rearrange("(kc p) n -> p kc n", p=128),
                         "w1b", ["w1_bf"], ["w1b"])
                self.dma("sp", w2[:], self.w2_bf[half * HF:(half + 1) * HF, :].rearrange("(kc p) n -> p kc n", p=128),
                         "w2", ["w2_bf"], ["w2"])
                blocks = self.tok_blocks(NB)
                if last:
                    blocks = blocks[1:]
                psi = 0
                ai = 0
                def prep(bi):
                    c0, n, v = blocks[bi]
                    s = bi % 2
                    h3 = bi % 3
                    self.dma("sp", hb3[h3][:, :, 0:n], hv[:, :, c0:c0 + n], ("hb", h3), ["hT"], [("hb", h3)])
                    self.dma("sp", u2[s][:, :, 0:n], uv[:, :, c0:c0 + n], ("u2", s), ["u2T"], [("u2", s)])
                prep(0)
                for bi, (c0, n, v) in enumerate(blocks):
                    s = bi % 2
                    h3 = bi % 3
                    if bi + 1 < len(blocks):
                        prep(bi + 1)
                    for fc in range(NH):
                        pa, pb = self.ps[psi % 4], self.ps[(psi + 1) % 4]
                        ka, kb = ("ps", psi % 4), ("ps", (psi + 1) % 4)
                        psi += 2
                        for kc in range(8):
                            self.mm(pa[:, 0:n], w1a[:, kc, fc * 128:(fc + 1) * 128], u2[s][:, kc, 0:n], kc == 0, kc == 7,
                                    ["w1a", ("u2", s)], [ka])
                        for kc in range(8):
                            self.mm(pb[:, 0:n], w1b[:, kc, fc * 128:(fc + 1) * 128], u2[s][:, kc, 0:n], kc == 0, kc == 7,
                                    ["w1b", ("u2", s)], [kb])
                        e = ai % 2
                        ai += 1
                        self.act(si_[e][:, 0:n], pa[:, 0:n], AF.Silu, [ka], [("si", e)])
                        self.tt("dve", ac[:, fc, 0:n], pb[:, 0:n], si_[e][:, 0:n], ALU.mult, [kb, ("si", e)], [("ac", fc)])
                    for oc in range(8):
                        po = self.ps[4 + oc % 2]
                        pk = ("ps", 4 + oc % 2)
                        for fc in range(NH):
                            self.mm(po[:, 0:n], w2[:, fc, oc * 128:(oc + 1) * 128], ac[:, fc, 0:n], fc == 0, fc == NH - 1,
                                    ["w2", ("ac", fc)], [pk])
                        self.stt(hb3[h3][:, oc, 0:n], po[:, 0:n], self.mod[:, l, 40 + oc, v:v + 1], hb3[h3][:, oc, 0:n], ALU.mult, ALU.add,
                                 [pk, ("hb", h3), ("mod", l)], [("hb", h3)])
                    if half == 1 and last:
                        self.norm_block(hb3[h3], ("hb", h3), n, sq, tmp, rt, lambda kc, n=n: un[0][:, kc, 0:n],
                                        lambda kc, v: self.gfin[:, kc:kc + 1], lambda kc, v: None, 0, "nf", "ob")
                        self.dma("sp", ov[:, :, c0 - CL:c0 - CL + n], un[0][:, :, 0:n], "ob", ["ob"], ["ob", "out"])
                    elif half == 1 and l + 1 < self.L:
                        ui = bi % 2
                        self.norm_block(hb3[h3], ("hb", h3), n, sq, tmp, rt, lambda kc, n=n, ui=ui: un[ui][:, kc, 0:n],
                                        lambda kc, v: self.gsc1[:, l + 1, kc, v:v + 1],
                                        lambda kc, v: self.mod[:, l + 1, 0 + kc, v:v + 1], v, "n1f", ("un", ui))
                        self.dma("sp", u1v[:, :, c0:c0 + n], un[ui][:, :, 0:n], ("un", ui), [("un", ui)], [("un", ui), "u1T"])
                    if not (half == 1 and last):
                        self.dma("sp", hv[:, :, c0:c0 + n], hb3[h3][:, :, 0:n], ("hb", h3), [("hb", h3)], ["hT"])

    def phase_final(self):
        nc = self.nc
        NB = 256
        with contextlib.ExitStack() as es:
            def sb(name, shape, dt):
                return es.enter_context(nc.sbuf_tensor("z_%s" % name, list(shape), dt))
            hb = [sb("hb%d" % i, [128, 8, NB], F32) for i in range(2)]
            ob = [sb("ob%d" % i, [128, 8, NB], F32) for i in range(2)]
            sq = sb("sq", [128, 8, NB], BF16)
            tmp = sb("tmp", [128, 8, NB], F32)
            rt = sb("rt", [128, NB], F32)
            hv = self.hT.rearrange("(kc p) t -> p kc t", p=128)
            ov = self.out_d.rearrange("(kc p) t -> p kc t", p=128)
            fblocks = self.tok_blocks(NB)[1:]
            self.dma("sp", hb[0][:], hv[:, :, fblocks[0][0]:fblocks[0][0] + NB], ("hb", 0), ["hT"], [("hb", 0)])
            for bi, (c0, n, v) in enumerate(fblocks):
                s = bi % 2
                if bi + 1 < len(fblocks):
                    c1 = fblocks[bi + 1][0]
                    self.dma("sp", hb[1 - s][:], hv[:, :, c1:c1 + NB], ("hb", 1 - s), ["hT"], [("hb", 1 - s)])
                self.norm_block(hb[s], ("hb", s), n, sq, tmp, rt, lambda kc, s=s: ob[s][:, kc, :],
                                lambda kc, v: self.gfin[:, kc:kc + 1], lambda kc, v: None, 0, "nf", ("ob", s))
                self.dma("sp", ov[:, :, c0 - CL:c0 - CL + n], ob[s][:], ("ob", s), [("ob", s)], ["out"])


def _rope_tables():
    rows = SEQ // 64
    r = np.repeat(np.arange(rows, dtype=np.float32), 64)
    col = np.tile(np.arange(64, dtype=np.float32), rows)
    inv = (np.float32(10000.0) ** (-np.arange(16, dtype=np.float32) / np.float32(16))).astype(np.float32)
    ang = np.concatenate([r[:, None] * inv, col[:, None] * inv], axis=-1).astype(np.float32)
    cos, sin = np.cos(ang).astype(np.float32), np.sin(ang).astype(np.float32)
    C = np.zeros((128, SEQ), np.float32)
    Sg = np.zeros((128, SEQ), np.float32)
    for p in range(128):
        d = p % 64
        jj = d % 32
        C[p] = cos[:, jj]
        Sg[p] = sin[:, jj] if d < 32 else -sin[:, jj]
    return C, Sg


def _consts():
    c = np.zeros((128, 3, 128), np.float32)
    c[:, 0, :] = np.eye(128, dtype=np.float32)
    for m in range(128):
        src = m + 32 if (m % 64) < 32 else m - 32
        c[src, 1, m] = 1.0
    c[:, 2, :] = 1.0 / 1024.0
    return c


def _pack_pp(inp):
    def chunks(vec):
        return np.ascontiguousarray(vec.reshape(-1, 128).T)
    pp = np.zeros((128, DEPTH, NPP), np.float32)
    for l in range(DEPTH):
        parts = {
            "b_in": chunks(inp["b_in"][l]),
            "g1": chunks(inp["g_norm1"][l]),
            "g2": chunks(inp["g_norm2"][l]),
            "rcw": inp["rnn_conv_w"][l].T.reshape(8, 128, 4).transpose(1, 0, 2).reshape(128, 32),
            "rcb": chunks(inp["rnn_conv_b"][l]),
            "rba": inp["rnn_b_a"][l].reshape(2, 8, 128).transpose(2, 0, 1).reshape(128, 16),
            "rbx": inp["rnn_b_x"][l].reshape(2, 8, 128).transpose(2, 0, 1).reshape(128, 16),
            "rlam": inp["rnn_lambda"][l].reshape(2, 8, 128).transpose(2, 0, 1).reshape(128, 16),
            "cdw": inp["conv_dw_w"][l].T.reshape(8, 128, 31).transpose(1, 0, 2).reshape(128, 248),
            "cdb": chunks(inp["conv_dw_b"][l]),
            "clg": chunks(inp["conv_ln_g"][l]),
            "clb": chunks(inp["conv_ln_b"][l]),
            "bmod": chunks(inp["b_mod"][l]),
            "gsub": inp["g_subln"][l].reshape(128, 1),
        }
        for name, n in PP_FIELDS:
            assert parts[name].shape == (128, n), (name, parts[name].shape)
            pp[:, l, PP_OFF[name]:PP_OFF[name] + n] = parts[name]
    return pp


def make_in_maps(inp):
    inp = {k: np.asarray(v) for k, v in inp.items()}
    C, Sg = _rope_tables()
    shared = {
        "pp": _pack_pp(inp),
        "gfin": np.ascontiguousarray(inp["g_final"].reshape(8, 128).T),
        "bv": np.ascontiguousarray(inp["b_in"][:, OFF_V:OFF_V + 1024]),
        "lq": np.ascontiguousarray(inp["lambda_qk"].reshape(1, DEPTH * 256)),
        "ropec": C, "ropes": Sg, "cst": _consts(),
    }
    for k in ("w_mod", "w_in", "rnn_w_a", "rnn_w_x", "w_rnn_o", "w_attn_o", "w_conv_o", "w_out", "w_ffn_in", "w_ffn_out"):
        shared[k] = np.ascontiguousarray(inp[k], dtype=np.float32)
    maps = []
    for b in range(NCORES):
        m = dict(shared)
        m["xin"] = np.ascontiguousarray(np.concatenate([inp["ctx"][b].T, inp["x"][b].T], axis=1), dtype=np.float32)
        cv = np.stack([inp["c"][b].reshape(8, 128).T, inp["c_ctx"].reshape(8, 128).T], axis=-1)
        m["cvec"] = np.ascontiguousarray(cv, dtype=np.float32)
        maps.append(m)
    return maps


_NC_CACHE = {}


def kernel(**inputs):
    if "nc" not in _NC_CACHE:
        _NC_CACHE["nc"] = Prog().build()
    nc = _NC_CACHE["nc"]
    maps = make_in_maps(inputs)
    res = run_bass_kernel_spmd(nc, maps, core_ids=list(range(NCORES)))
    out = np.stack([np.ascontiguousarray(res.results[b]["out"].T) for b in range(NCORES)], axis=0)
    return out.astype(np.float32)
```

```python
import contextlib
import math
import numpy as np
import concourse.bass as bass
import concourse.mybir as mybir
from concourse.bass_utils import run_bass_kernel_spmd

F32 = mybir.dt.float32
BF16 = mybir.dt.bfloat16
AF = mybir.ActivationFunctionType
ALU = mybir.AluOpType

D = 1024
SEQ = 4096
CL = 256
T = CL + SEQ
DEPTH = 4
NCORES = 8
EPS = 1e-6
DFF = 2816
NFC = DFF // 128
OFF_RX, OFF_K, OFF_V, OFF_RG, OFF_Q, OFF_CV, OFF_CG, OFF_G = 0, 1024, 2048, 3072, 4096, 5120, 6144, 7168
IN_COLS = 10240

PP_FIELDS = [("b_in", 80), ("g1", 8), ("g2", 8), ("rcw", 32), ("rcb", 8), ("rba", 16), ("rbx", 16),
             ("rlam", 16), ("cdw", 248), ("cdb", 8), ("clg", 8), ("clb", 8), ("bmod", 48), ("gsub", 1)]
PP_OFF = {}
_o = 0
for _n, _c in PP_FIELDS:
    PP_OFF[_n] = _o
    _o += _c
NPP = _o


class Sched:
    ENGS = ("pe", "act", "dve", "pool", "sp")

    def __init__(self, nc):
        self.nc = nc
        self.q = {e: [] for e in self.ENGS}
        self.cnt = {}
        self.waited = {}
        self.last_w = {}
        self.readers = {}
        self.sems = {}

    def _sem(self, key):
        if key not in self.sems:
            self.sems[key] = self.nc.alloc_semaphore(name="s_%d" % len(self.sems))
            self.cnt[key] = 0
        return self.sems[key]

    def _deps(self, eng, reads, writes):
        deps = {}
        for r in reads:
            d = self.last_w.get(r)
            if d is not None and deps.get(d[0], 0) < d[1]:
                deps[d[0]] = d[1]
        for w in writes:
            d = self.last_w.get(w)
            if d is not None and deps.get(d[0], 0) < d[1]:
                deps[d[0]] = d[1]
            rd = self.readers.get(w)
            if rd:
                for k, v in rd.items():
                    if deps.get(k, 0) < v:
                        deps[k] = v
        out = []
        for k, v in deps.items():
            if k == ("eng", "pe") and eng == "pe":
                continue
            if self.waited.get((eng, k), 0) < v:
                self.waited[(eng, k)] = v
                out.append((self._sem(k), v))
        return out

    def _commit(self, semkey, val, reads, writes):
        for r in reads:
            self.readers.setdefault(r, {})[semkey] = val
        for w in writes:
            self.last_w[w] = (semkey, val)
            self.readers[w] = {}

    def op(self, eng, fn, reads=(), writes=()):
        waits = self._deps(eng, reads, writes)
        semkey = ("eng", eng)
        sem = self._sem(semkey)
        self.cnt[semkey] += 1
        self.q[eng].append((waits, fn, sem, 1))
        self._commit(semkey, self.cnt[semkey], reads, writes)

    def dma(self, eng, fn, semkey, reads=(), writes=()):
        waits = self._deps(eng, reads, writes)
        semkey = ("dma", semkey)
        sem = self._sem(semkey)
        self.cnt[semkey] += 16
        self.q[eng].append((waits, fn, sem, 16))
        self._commit(semkey, self.cnt[semkey], reads, writes)

    def barrier(self):
        for eng in self.ENGS:
            waits = []
            for k, v in self.cnt.items():
                if v > 0 and self.waited.get((eng, k), 0) < v and k != ("eng", eng):
                    self.waited[(eng, k)] = v
                    waits.append((self._sem(k), v))
            if waits:
                self.q[eng].append((waits, None, None, 0))
        self.last_w = {}
        self.readers = {}

    def emit(self, final_eng="sp"):
        nc = self.nc
        waits = [(self._sem(k), v) for k, v in self.cnt.items() if v > 0 and k != ("eng", final_eng)]
        self.q[final_eng].append((waits, None, None, 0))
        hmap = {"pe": "tensor", "act": "scalar", "dve": "vector", "pool": "gpsimd", "sp": "sync"}
        with nc.Block() as block:
            for eng in self.ENGS:
                def body(e, items=self.q[eng]):
                    for waits, fn, sem, inc in items:
                        for s, v in waits:
                            e.wait_ge(s, v)
                        if fn is not None:
                            fn(e).then_inc(sem, inc)
                getattr(block, hmap[eng])(body)


class Prog:
    def __init__(self, n_layers=DEPTH, debug=False, nphase=99):
        self.nphase = nphase
        self.L = n_layers
        self.debug = debug
        self.nc = bass.Bass("TRN2", target_bir_lowering=False)
        self.S = Sched(self.nc)
        self.uid = 0

    def dram(self, name, shape, dt, kind="Internal"):
        return self.nc.dram_tensor(name, list(shape), dt, kind=kind).ap()

    def mm(self, out, lhsT, rhs, start, stop, reads, writes):
        self.S.op("pe", lambda e: e.matmul(out, lhsT, rhs, start=start, stop=stop), reads, writes)

    def act(self, out, in_, func, reads, writes, bias=0.0, scale=1.0, accum_out=None):
        if accum_out is None:
            self.S.op("act", lambda e: e.activation(out=out, in_=in_, func=func, bias=bias, scale=scale), reads, writes)
        else:
            self.S.op("act", lambda e: e.activation(out=out, in_=in_, func=func, bias=bias, scale=scale,
                                                    accum_out=accum_out), reads, writes)

    def tt(self, eng, out, in0, in1, op, reads, writes):
        self.S.op(eng, lambda e: e.tensor_tensor(out=out, in0=in0, in1=in1, op=op), reads, writes)

    def ts(self, eng, out, in0, s1, op0, reads, writes, s2=None, op1=None):
        if op1 is None:
            self.S.op(eng, lambda e: e.tensor_scalar(out=out, in0=in0, scalar1=s1, scalar2=None, op0=op0), reads, writes)
        else:
            self.S.op(eng, lambda e: e.tensor_scalar(out=out, in0=in0, scalar1=s1, scalar2=s2, op0=op0, op1=op1), reads, writes)

    def stt(self, out, in0, scalar, in1, op0, op1, reads, writes):
        self.S.op("dve", lambda e: e.scalar_tensor_tensor(out=out, in0=in0, scalar=scalar, in1=in1, op0=op0, op1=op1),
                  reads, writes)

    def copy(self, eng, out, in_, reads, writes):
        if eng == "act":
            self.S.op("act", lambda e: e.copy(out=out, in_=in_), reads, writes)
        else:
            self.S.op(eng, lambda e: e.tensor_copy(out=out, in_=in_), reads, writes)

    def recip(self, out, in_, reads, writes):
        self.S.op("dve", lambda e: e.reciprocal(out=out, in_=in_), reads, writes)

    def memset(self, eng, ap, val, writes):
        self.S.op(eng, lambda e: e.memset(ap, val), (), writes)

    def dma(self, eng, out, in_, key, reads, writes):
        self.S.dma(eng, lambda e: e.dma_start(out=out, in_=in_), key, reads, writes)

    def build(self):
        nc = self.nc
        L = self.L
        In = "ExternalInput"
        self.xin = self.dram("xin", [D, T], F32, In)
        self.cvec = self.dram("cvec", [128, 8, 2], F32, In)
        self.pp_d = self.dram("pp", [128, DEPTH, NPP], F32, In)
        self.gfin_d = self.dram("gfin", [128, 8], F32, In)
        self.bv_d = self.dram("bv", [DEPTH, 1024], F32, In)
        self.lq_d = self.dram("lq", [1, DEPTH * 256], F32, In)
        self.cos_d = self.dram("ropec", [128, SEQ], F32, In)
        self.sin_d = self.dram("ropes", [128, SEQ], F32, In)
        self.cst_d = self.dram("cst", [128, 3, 128], F32, In)
        self.w_mod = self.dram("w_mod", [DEPTH, D, 6 * D], F32, In)
        self.w_in = self.dram("w_in", [DEPTH, D, IN_COLS], F32, In)
        self.rnn_w_a = self.dram("rnn_w_a", [DEPTH, 2, 16, 64, 64], F32, In)
        self.rnn_w_x = self.dram("rnn_w_x", [DEPTH, 2, 16, 64, 64], F32, In)
        self.w_rnn_o = self.dram("w_rnn_o", [DEPTH, D, D], F32, In)
        self.w_attn_o = self.dram("w_attn_o", [DEPTH, D, D], F32, In)
        self.w_conv_o = self.dram("w_conv_o", [DEPTH, D, D], F32, In)
        self.w_out = self.dram("w_out", [DEPTH, D, D], F32, In)
        self.w_ffn_in = self.dram("w_ffn_in", [DEPTH, D, 2 * DFF], F32, In)
        self.w_ffn_out = self.dram("w_ffn_out", [DEPTH, DFF, D], F32, In)
        self.out_d = self.dram("out", [D, SEQ], F32, "ExternalOutput")
        dbg = "ExternalOutput" if self.debug else "Internal"
        self.hT = self.dram("hT", [D, T], F32, dbg)
        self.rxT = self.dram("rxT", [D, T], BF16, dbg)
        self.rgT = self.dram("rgT", [D, T], BF16, dbg)
        self.kT = self.dram("kT", [D, T], BF16, dbg)
        self.qT = self.dram("qT", [D, T], BF16, dbg)
        self.zT = self.dram("zT", [D, T], BF16, dbg)
        self.zcT = self.dram("zcT", [D, T], BF16, dbg)
        self.trT = self.dram("trT", [D, T], BF16, dbg)
        self.oT = self.dram("oT", [D, T], BF16, dbg)
        self.gT = self.dram("gT", [3 * D, T], BF16, dbg)
        self.Vs = self.dram("Vs", [T, D], BF16, dbg)
        self.u2T = self.dram("u2T", [D, T], BF16, "Internal")
        self.u1T = self.dram("u1T", [D, T], BF16, "Internal")
        self.wm_bf = self.dram("wm_bf", [4, D, D], BF16, "Internal")
        self.w1_bf = self.dram("w1_bf", [D, 2 * DFF], BF16, "Internal")
        self.w2_bf = self.dram("w2_bf", [DFF, D], BF16, "Internal")

        with contextlib.ExitStack() as top:
            def sb(name, shape, dt):
                return top.enter_context(nc.sbuf_tensor(name, list(shape), dt))
            self.ps = [top.enter_context(nc.psum_tensor("ps%d" % i, [128, 512], F32)) for i in range(7)]
            self.psb = top.enter_context(nc.psum_tensor("psb", [128, 1024], BF16))
            self.pp = sb("pp_sb", [128, DEPTH, NPP], F32)
            self.cstf = sb("cstf", [128, 3, 128], F32)
            self.cstb = sb("cstb", [128, 3, 128], BF16)
            self.mod = sb("mod", [128, DEPTH, 48, 2], F32)
            self.gsc1 = sb("gsc1", [128, DEPTH, 8, 2], F32)
            self.gsc2 = sb("gsc2", [128, DEPTH, 8, 2], F32)
            self.cdec = sb("cdec", [128, DEPTH, 16], F32)
            self.nlam = sb("nlam", [128, DEPTH], F32)
            self.gsub = sb("gsubs", [128, DEPTH], F32)
            self.gfin = sb("gfin_sb", [128, 8], F32)
            self.prologue()
            np_ = 0
            for l in range(L):
                for ph in (self.phase_proj, self.phase_rc, self.phase_attn_merge, self.phase_ffn):
                    if np_ < self.nphase:
                        self.S.barrier()
                        ph(l)
                    np_ += 1
            if np_ <= self.nphase and L < DEPTH:
                self.S.barrier()
                self.phase_final()
            self.S.emit()
        return nc

    def ppc(self, l, name, idx=0):
        o = PP_OFF[name] + idx
        return self.pp[:, l, o:o + 1]

    def prologue(self):
        nc, S = self.nc, self.S
        self.dma("sp", self.pp[:], self.pp_d, "pp", [], ["pp"])
        self.dma("sp", self.cstf[:], self.cst_d, "cst", [], ["cstf"])
        self.dma("sp", self.gfin[:], self.gfin_d, "gfin", [], ["gfin"])
        self.copy("dve", self.cstb[:], self.cstf[:], ["cstf"], ["cstb"])
        with contextlib.ExitStack() as es:
            def sb(name, shape, dt):
                return es.enter_context(nc.sbuf_tensor(name, list(shape), dt))
            cv = sb("p_cv", [128, 8, 2], F32)
            sc = sb("p_sc", [128, 8, 2], F32)
            wm = [sb("p_wm%d" % i, [128, 8, 768], F32) for i in range(2)]
            lqb = sb("p_lq", [128, DEPTH, 256], F32)
            lt = sb("p_lt", [128, 2, 64], F32)
            ls = sb("p_ls", [128, DEPTH, 2], F32)
            tmp16 = sb("p_t16", [128, DEPTH, 16], F32)
            self.dma("sp", cv[:], self.cvec, "cv", [], ["cv"])
            self.act(sc[:], cv[:], AF.Silu, ["cv"], ["sc"])
            it = 0
            for l in range(self.L):
                for g in range(8):
                    slot = it % 2
                    it += 1
                    src = self.w_mod[l, :, g * 768:(g + 1) * 768].rearrange("(kc p) n -> p kc n", p=128)
                    self.dma("sp", wm[slot][:], src, ("wm", slot), [], [("wm", slot)])
                    for cc in range(6):
                        ch = g * 6 + cc
                        pst = self.ps[cc % 4]
                        for kc in range(8):
                            self.mm(pst[:, 0:2], wm[slot][:, kc, cc * 128:(cc + 1) * 128], sc[:, kc, :],
                                    kc == 0, kc == 7, [("wm", slot), "sc"], [("ps", cc % 4)])
                        self.ts("dve", self.mod[:, l, ch, :], pst[:, 0:2], self.ppc(l, "bmod", ch), ALU.add,
                                [("ps", cc % 4), "pp"], [("mod", l)])
            for l in range(self.L):
                for v in range(2):
                    for (dst, gname, base) in ((self.gsc1, "g1", 8), (self.gsc2, "g2", 32)):
                        o = PP_OFF[gname]
                        self.stt(dst[:, l, :, v], self.mod[:, l, base:base + 8, v], 1.0, self.pp[:, l, o:o + 8],
                                 ALU.add, ALU.mult, [("mod", l), "pp"], [("gsc", l)])
            for l in range(self.L):
                o = PP_OFF["rlam"]
                self.act(tmp16[:, l, :], self.pp[:, l, o:o + 16], AF.Exp, ["pp"], [("t16", l)], scale=-1.0)
                self.act(tmp16[:, l, :], tmp16[:, l, :], AF.Ln, [("t16", l)], [("t16", l)], bias=1.0)
                self.ts("dve", self.cdec[:, l, :], tmp16[:, l, :], -8.0, ALU.mult, [("t16", l)], [("cdec", l)])
            self.dma("sp", lqb[:].rearrange("p l n -> p (l n)"),
                     self.lq_d.partition_broadcast(128), "lq", [], ["lq"])
            for l in range(self.L):
                lam_init = 0.8 - 0.6 * math.exp(-0.3 * l)
                for i in range(2):
                    self.tt("dve", lt[:, i, :], lqb[:, l, (2 * i) * 64:(2 * i + 1) * 64],
                            lqb[:, l, (2 * i + 1) * 64:(2 * i + 2) * 64], ALU.mult, ["lq"], ["lt"])
                    self.S.op("dve", lambda e, i=i, l=l: e.tensor_reduce(out=ls[:, l, i:i + 1], in_=lt[:, i, :],
                                                                          axis=mybir.AxisListType.X, op=ALU.add),
                              ["lt"], [("ls", l)])
                self.act(ls[:, l, :], ls[:, l, :], AF.Exp, [("ls", l)], [("ls", l)])
                self.tt("dve", self.nlam[:, l:l + 1], ls[:, l, 1:2], ls[:, l, 0:1], ALU.subtract, [("ls", l)], [("nlam", l)])
                self.ts("dve", self.nlam[:, l:l + 1], self.nlam[:, l:l + 1], -lam_init, ALU.add, [("nlam", l)], [("nlam", l)])
                self.ts("dve", self.gsub[:, l:l + 1], self.ppc(l, "gsub"), 1.0 - lam_init, ALU.mult, ["pp"], [("gsub", l)])
            if self.debug:
                d_mod = self.dram("d_mod", [128, DEPTH * 48 * 2], F32, "ExternalOutput")
                d_misc = self.dram("d_misc", [128, DEPTH * 16 + 2 * DEPTH], F32, "ExternalOutput")
                rd = [("mod", l) for l in range(self.L)] + [("cdec", l) for l in range(self.L)] + \
                     [("nlam", l) for l in range(self.L)] + [("gsub", l) for l in range(self.L)]
                self.dma("sp", d_mod, self.mod[:].rearrange("p a b c -> p (a b c)"), "d1", rd, [])
                self.dma("sp", d_misc[:, 0:DEPTH * 16], self.cdec[:].rearrange("p a b -> p (a b)"), "d2", rd, [])
                self.dma("sp", d_misc[:, DEPTH * 16:DEPTH * 17], self.nlam[:], "d3", rd, [])
                self.dma("sp", d_misc[:, DEPTH * 17:DEPTH * 18], self.gsub[:], "d4", rd, [])
            self.S.barrier()

    def norm_block(self, hb, hkey, n, sq, tmp, rt, out_fn, gsc, sh_fn, v, pk, okey, offload=False):
        hkeys = list(hkey) if isinstance(hkey, list) else [hkey]
        onesb = self.cstb[:, 2, :]
        if offload:
            self.tt("pool", sq[:, :, 0:n], hb[:, :, 0:n], hb[:, :, 0:n], ALU.mult, hkeys, [pk + "sq"])
        else:
            self.act(sq[:, :, 0:n], hb[:, :, 0:n], AF.Square, hkeys, [pk + "sq"])
        pst = self.ps[6]
        for kc in range(8):
            self.mm(pst[:, 0:n], onesb, sq[:, kc, 0:n], kc == 0, kc == 7, [pk + "sq", "cstb"], [("ps", 6)])
        self.act(rt[:, 0:n], pst[:, 0:n], AF.Sqrt, [("ps", 6)], [pk + "rt"], bias=EPS)
        self.recip(rt[:, 0:n], rt[:, 0:n], [pk + "rt"], [pk + "rt"])
        for kc in range(8):
            self.stt(tmp[:, kc, 0:n], hb[:, kc, 0:n], gsc(kc, v), rt[:, 0:n], ALU.mult, ALU.mult,
                     hkeys + [pk + "rt"], [(pk + "tmp", kc)])
            sh = sh_fn(kc, v)
            if offload:
                if sh is None:
                    self.ts("pool", out_fn(kc), tmp[:, kc, 0:n], 1.0, ALU.mult, [(pk + "tmp", kc)], [okey], s2=1.0, op1=ALU.mult)
                else:
                    self.ts("pool", out_fn(kc), tmp[:, kc, 0:n], sh, ALU.add, [(pk + "tmp", kc)], [okey], s2=1.0, op1=ALU.mult)
            elif sh is None:
                self.act(out_fn(kc), tmp[:, kc, 0:n], AF.Identity, [(pk + "tmp", kc)], [okey])
            else:
                self.act(out_fn(kc), tmp[:, kc, 0:n], AF.Identity, [(pk + "tmp", kc)], [okey], bias=sh)

    def tok_blocks(self, n):
        out = [(0, CL, 1)]
        for c0 in range(CL, T, n):
            out.append((c0, n, 0))
        return out

    def phase_proj(self, l):
        nc = self.nc
        hsrc = self.xin if l == 0 else self.hT
        with contextlib.ExitStack() as es:
            def sb(name, shape, dt):
                return es.enter_context(nc.sbuf_tensor("a%d_%s" % (l, name), list(shape), dt))
            uT = sb("uT", [128, 8, T], BF16)
            hsrc_v = hsrc.rearrange("(kc p) t -> p kc t", p=128)
            if l > 0:
                self.dma("sp", uT[:], self.u1T.rearrange("(kc p) t -> p kc t", p=128), "uT", ["u1T"], ["uT"])
            with contextlib.ExitStack() as es2:
                def sb2(name, shape, dt):
                    return es2.enter_context(nc.sbuf_tensor("a%d_%s" % (l, name), list(shape), dt))
                hb = [sb2("hb%d" % i, [128, 8, 256], F32) for i in range(2)]
                sq = sb2("sq", [128, 8, 256], BF16)
                tmp = sb2("tmp", [128, 8, 256], F32)
                rt = sb2("rt", [128, 256], F32)
                for bi, (c0, n, isctx) in enumerate(self.tok_blocks(256) if l == 0 else []):
                    s = bi % 2
                    self.dma("sp", hb[s][:], hsrc_v[:, :, c0:c0 + n], ("hb", s), ["hT"], [("hb", s)])
                    self.norm_block(hb[s], ("hb", s), n, sq, tmp, rt,
                                    lambda kc, c0=c0, n=n: uT[:, kc, c0:c0 + n],
                                    lambda kc, v: self.gsc1[:, l, kc, v:v + 1],
                                    lambda kc, v: self.mod[:, l, 0 + kc, v:v + 1], isctx, "n1", "uT")
                self.S.barrier()
            w = [sb("w%d" % i, [128, 8, 1024], BF16) for i in range(2)]
            stage = [sb("st%d" % i, [128, T], BF16) for i in range(2)]
            tcb = [sb("tcb_%d" % i, [128, 512], BF16) for i in range(3)]
            tsb = [sb("tsb_%d" % i, [128, 512], BF16) for i in range(3)]
            t16 = [sb("t16_%d" % i, [128, 512], BF16) for i in range(3)]
            ra = [sb("ra%d" % i, [128, 512], F32) for i in range(3)]
            cs = sb("cs", [128, 2, SEQ], BF16)
            for hh in range(2):
                self.dma("pool", cs[:, 0, hh * 2048:(hh + 1) * 2048], self.cos_d[:, hh * 2048:(hh + 1) * 2048], "cs", [], ["cs"])
                self.dma("pool", cs[:, 1, hh * 2048:(hh + 1) * 2048], self.sin_d[:, hh * 2048:(hh + 1) * 2048], "cs", [], ["cs"])
            bvb = sb("bvb", [128, 1024], F32)
            vst = [sb("vst%d" % i, [128, 1024], BF16) for i in range(2)]
            stages = [("RX", [(OFF_RX, 1024)]), ("K", [(OFF_K, 1024)]), ("V", [(OFF_V, 1024)]),
                      ("RG", [(OFF_RG, 1024)]), ("Q", [(OFF_Q, 1024)]),
                      ("CVG0", [(OFF_CV, 512), (OFF_CG, 512)]), ("CVG1", [(OFF_CV + 512, 512), (OFF_CG + 512, 512)]),
                      ("G0", [(OFF_G, 1024)]), ("G1", [(OFF_G + 1024, 1024)]), ("G2", [(OFF_G + 2048, 1024)])]
            self.dma("sp", bvb[:], self.bv_d[l:l + 1, :].partition_broadcast(128), "bvb", [], ["bvb"])
            blocks = self.tok_blocks(512)
            psi = 0
            sti = 0
            csi = 0
            ti = 0

            def load_w(si):
                slot = si % 2
                off = 0
                for (c0, n) in stages[si][1]:
                    src = self.w_in[l, :, c0:c0 + n].rearrange("(kc p) n -> p kc n", p=128)
                    self.dma("pool", w[slot][:, :, off:off + n], src, ("w", slot), [], [("w", slot)])
                    off += n
            load_w(0)
            dq = []
            for si, (sname, _) in enumerate(stages):
                slot = si % 2
                wk = ("w", slot)
                if si + 1 < len(stages):
                    load_w(si + 1)
                if sname == "V":
                    for tc in range(T // 128):
                        vs = tc % 2
                        for half in range(2):
                            pst = self.ps[psi % 4]
                            pk = ("ps", psi % 4)
                            psi += 1
                            for kc in range(8):
                                self.mm(pst[:, :], uT[:, kc, tc * 128:(tc + 1) * 128], w[slot][:, kc, half * 512:(half + 1) * 512],
                                        kc == 0, kc == 7, ["uT", wk], [pk])
                            self.tt("dve", vst[vs][:, half * 512:(half + 1) * 512], pst[:, :], bvb[:, half * 512:(half + 1) * 512],
                                    ALU.add, [pk, "bvb"], [("vst", vs)])
                        self.dma("sp", self.Vs[tc * 128:(tc + 1) * 128, :], vst[vs][:], ("vst", vs), [("vst", vs)], ["Vs"])
                    continue
                if sname.startswith("CVG"):
                    half = int(sname[3])
                    for jj in range(4):
                        j = half * 4 + jj
                        st = stage[sti % 2]
                        stk = ("stage", sti % 2)
                        sti += 1
                        for (c0, n, isctx) in blocks:
                            pa, pb = self.ps[psi % 4], self.ps[(psi + 1) % 4]
                            pka, pkb = ("ps", psi % 4), ("ps", (psi + 1) % 4)
                            psi += 2
                            for kc in range(8):
                                self.mm(pa[:, 0:n], w[slot][:, kc, jj * 128:(jj + 1) * 128], uT[:, kc, c0:c0 + n],
                                        kc == 0, kc == 7, ["uT", wk], [pka])
                            for kc in range(8):
                                self.mm(pb[:, 0:n], w[slot][:, kc, 512 + jj * 128:512 + (jj + 1) * 128], uT[:, kc, c0:c0 + n],
                                        kc == 0, kc == 7, ["uT", wk], [pkb])
                            r = ra[ti % 2]
                            rk = ("ra", ti % 2)
                            ti += 1
                            self.act(r[:, 0:n], pb[:, 0:n], AF.Sigmoid, [pkb, "pp"], [rk], bias=self.ppc(l, "b_in", OFF_CG // 128 + j))
                            self.stt(st[:, c0:c0 + n], pa[:, 0:n], self.ppc(l, "b_in", OFF_CV // 128 + j), r[:, 0:n],
                                     ALU.add, ALU.mult, [pka, rk, "pp"], [(stk, c0)])
                        self.dma("sp", self.zT[j * 128:(j + 1) * 128, :], st[:], stk, [(stk, b_[0]) for b_ in blocks], ["zT"])
                    continue
                off = stages[si][1][0][0]
                for j in range(8):
                    st = stage[sti % 2]
                    stk = ("stage", sti % 2)
                    sti += 1
                    bias = self.ppc(l, "b_in", off // 128 + j)
                    for (c0, n, isctx) in blocks:
                        pst = self.ps[psi % 4]
                        pk = ("ps", psi % 4)
                        psi += 1
                        for kc in range(8):
                            self.mm(pst[:, 0:n], w[slot][:, kc, j * 128:(j + 1) * 128], uT[:, kc, c0:c0 + n],
                                    kc == 0, kc == 7, ["uT", wk], [pk])
                        while len(dq) > 1:
                            dq.pop(0)()
                        if sname in ("K", "Q") and not isctx:
                            a16, tc_, ts_ = t16[ti % 3], tcb[ti % 3], tsb[ti % 3]
                            k16, kc_k, ks_k = ("t16", ti % 3), ("tcb", ti % 3), ("tsb", ti % 3)
                            ti += 1
                            q0 = c0 - CL
                            c = cs[:, :, q0:q0 + n]
                            ck = "cs"
                            self.act(a16[:, 0:n], pst[:, 0:n], AF.Identity, [pk, "pp"], [k16], bias=bias)
                            psw = self.ps[4 + (ti % 2)]
                            pswk = ("ps", 4 + (ti % 2))

                            def deferred_fn(a16=a16, tc_=tc_, ts_=ts_, k16=k16, kc_k=kc_k, ks_k=ks_k, c=c, ck=ck, psw=psw, pswk=pswk,
                                         n=n, c0=c0, st=st, stk=stk):
                                self.tt("dve", tc_[:, 0:n], a16[:, 0:n], c[:, 0, 0:n], ALU.mult, [k16, ck], [kc_k])
                                self.tt("dve", ts_[:, 0:n], a16[:, 0:n], c[:, 1, 0:n], ALU.mult, [k16, ck], [ks_k])
                                self.mm(psw[:, 0:n], self.cstb[:, 0, :], tc_[:, 0:n], True, False, [kc_k, "cstb"], [pswk])
                                self.mm(psw[:, 0:n], self.cstb[:, 1, :], ts_[:, 0:n], False, True, [ks_k, "cstb"], [pswk])
                                self.act(st[:, c0:c0 + n], psw[:, 0:n], AF.Identity, [pswk], [(stk, c0)])
                            dq.append(deferred_fn)
                        else:
                            func = {"RX": AF.Identity, "K": AF.Identity, "Q": AF.Identity, "RG": AF.Gelu_apprx_tanh}.get(sname, AF.Sigmoid)
                            self.act(st[:, c0:c0 + n], pst[:, 0:n], func, [pk, "pp"], [(stk, c0)], bias=bias)
                    while dq:
                        dq.pop(0)()
                    dst = {"RX": self.rxT, "K": self.kT, "Q": self.qT, "RG": self.rgT}.get(sname)
                    if dst is None:
                        gi = int(sname[1])
                        dst_ap = self.gT[(gi * 8 + j) * 128:(gi * 8 + j + 1) * 128, :]
                        dk = "gT"
                    else:
                        dst_ap = dst[j * 128:(j + 1) * 128, :]
                        dk = sname + "T"
                    self.dma("sp", dst_ap, st[:], stk, [(stk, b_[0]) for b_ in blocks], [dk])

    def phase_rglru(self, l):
        nc = self.nc
        WP = T + 6
        WO = T + 3
        with contextlib.ExitStack() as es:
            def sb(name, shape, dt):
                return es.enter_context(nc.sbuf_tensor("b%d_%s" % (l, name), list(shape), dt))
            rxp = [sb("rxp%d" % i, [128, WP], BF16) for i in range(2)]
            gel = [sb("gel%d" % i, [128, T], BF16) for i in range(2)]
            tro = [sb("tro%d" % i, [128, T], BF16) for i in range(2)]
            xr = sb("xr", [128, WO], F32)
            xrb = sb("xrb", [128, WO], BF16)
            rb = sb("r", [128, WO], F32)
            ib = sb("i", [128, WO], F32)
            tb = sb("t", [128, WO], F32)
            hf = sb("hf", [128, WO], F32)
            bd = sb("bd", [128, 8, 4, 128], BF16)
            dg = [sb("dg%d" % i, [128, 4, 128], BF16) for i in range(2)]
            for i in range(2):
                self.memset("pool", rxp[i][:], 0.0, [("rxp", i)])
            self.memset("pool", bd[:], 0.0, ["bd"])
            for d in range(2):
                for g, wsrc in enumerate((self.rnn_w_a, self.rnn_w_x)):
                    for half in range(2):
                        src = wsrc[l, d].rearrange("(j h) i o -> h i j o", h=2)[half]
                        self.dma("pool", bd[half * 64:(half + 1) * 64, :, d * 2 + g, half * 64:(half + 1) * 64], src,
                                 "bd", [], ["bd"])
            identb = self.cstb[:, 0, :]
            oblocks = [(c0, min(512, WO - c0)) for c0 in range(0, WO, 512)]
            psi = 0
            def load_chunk(j):
                s = j % 2
                self.dma("sp", rxp[s][:, 2:2 + CL], self.rxT[j * 128:(j + 1) * 128, 0:CL], ("rxp", s), ["rxT"], [("rxp", s)])
                self.dma("sp", rxp[s][:, 261:261 + SEQ], self.rxT[j * 128:(j + 1) * 128, CL:T], ("rxp", s), ["rxT"], [("rxp", s)])
                self.dma("sp", gel[s][:], self.rgT[j * 128:(j + 1) * 128, :], ("gel", s), ["rgT"], [("gel", s)])
            load_chunk(0)
            for j in range(8):
                s = j % 2
                if j + 1 < 8:
                    load_chunk(j + 1)
                for k in range(4):
                    self.ts("dve", dg[s][:, k, :], identb, self.ppc(l, "rcw", j * 4 + k), ALU.mult, ["cstb", "pp"], [("dg", s)])
                for (c0, n) in oblocks:
                    pst = self.ps[psi % 4]
                    pk = ("ps", psi % 4)
                    psi += 1
                    for k in range(4):
                        self.mm(pst[:, 0:n], dg[s][:, k, :], rxp[s][:, c0 + k:c0 + k + n], k == 0, k == 3,
                                [("dg", s), ("rxp", s)], [pk])
                    self.act(xr[:, c0:c0 + n], pst[:, 0:n], AF.Identity, [pk, "pp"], ["xr"], bias=self.ppc(l, "rcb", j))
                self.copy("dve", xrb[:], xr[:], ["xr"], ["xrb"])
                for d in range(2):
                    for g, (dst, bname, dk) in enumerate(((rb, "rba", "r"), (ib, "rbx", "i"))):
                        for (c0, n) in oblocks:
                            pst = self.ps[psi % 4]
                            pk = ("ps", psi % 4)
                            psi += 1
                            self.mm(pst[:, 0:n], bd[:, j, d * 2 + g, :], xrb[:, c0:c0 + n], True, True, ["bd", "xrb"], [pk])
                            self.act(dst[:, c0:c0 + n], pst[:, 0:n], AF.Sigmoid, [pk, "pp"], [dk],
                                     bias=self.ppc(l, bname, d * 8 + j))
                    self.act(rb[:], rb[:], AF.Exp, ["r", ("cdec", l)], ["r"], scale=self.cdec[:, l, d * 8 + j:d * 8 + j + 1])
                    self.tt("dve", tb[:], rb[:], rb[:], ALU.mult, ["r"], ["t"])
                    self.act(tb[:], tb[:], AF.Sqrt, ["t"], ["t"], bias=1.0, scale=-1.0)
                    self.tt("pool", ib[:], ib[:], xr[:], ALU.mult, ["i", "xr"], ["i"])
                    self.tt("dve", ib[:], ib[:], tb[:], ALU.mult, ["i", "t"], ["i"])
                    dst = hf if d == 0 else tb
                    dk = "hf" if d == 0 else "t"
                    if d == 0:
                        self.S.op("dve", lambda e, dst=dst: e.tensor_tensor_scan(
                            out=dst[:, 0:CL], data0=rb[:, 0:CL], data1=ib[:, 0:CL], initial=0.0, op0=ALU.mult, op1=ALU.add),
                            ["r", "i"], [dk])
                        self.S.op("dve", lambda e, dst=dst: e.tensor_tensor_scan(
                            out=dst[:, 259:WO], data0=rb[:, 259:WO], data1=ib[:, 259:WO], initial=dst[:, CL - 1:CL],
                            op0=ALU.mult, op1=ALU.add), ["r", "i", dk], [dk])
                    else:
                        self.S.op("dve", lambda e, dst=dst: e.tensor_tensor_scan(
                            out=dst[:, 0:CL][:, ::-1], data0=rb[:, 0:CL][:, ::-1], data1=ib[:, 0:CL][:, ::-1], initial=0.0,
                            op0=ALU.mult, op1=ALU.add), ["r", "i", "t"], [dk])
                        self.S.op("dve", lambda e, dst=dst: e.tensor_tensor_scan(
                            out=dst[:, 259:WO][:, ::-1], data0=rb[:, 259:WO][:, ::-1], data1=ib[:, 259:WO][:, ::-1],
                            initial=dst[:, 0:1], op0=ALU.mult, op1=ALU.add), ["r", "i", dk], [dk])
                self.tt("pool", hf[:], hf[:], tb[:], ALU.add, ["hf", "t"], ["hf"])
                self.tt("dve", tro[s][:, 0:CL], hf[:, 0:CL], gel[s][:, 0:CL], ALU.mult, ["hf", ("gel", s)], [("tro", s)])
                self.tt("dve", tro[s][:, CL:T], hf[:, 259:WO], gel[s][:, CL:T], ALU.mult, ["hf", ("gel", s)], [("tro", s)])
                self.dma("sp", self.trT[j * 128:(j + 1) * 128, :], tro[s][:], ("tro", s), [("tro", s)], ["trT"])

    def phase_conv(self, l):
        nc = self.nc
        WP = 15 + CL + 30 + SEQ + 15
        with contextlib.ExitStack() as es:
            def sb(name, shape, dt):
                return es.enter_context(nc.sbuf_tensor("c%d_%s" % (l, name), list(shape), dt))
            zp = [sb("zp%d" % i, [128, WP], BF16) for i in range(2)]
            zo = [sb("zo%d" % i, [128, T], BF16) for i in range(2)]
            dg = [sb("dg%d" % i, [128, 31, 128], BF16) for i in range(2)]
            for i in range(2):
                self.memset("pool", zp[i][:], 0.0, [("zp", i)])
            identb = self.cstb[:, 0, :]
            blocks = [(0, CL, 0)] + [(286 + 512 * i, 512, CL + 512 * i) for i in range(8)]
            psi = 0
            def load_chunk(j):
                s = j % 2
                self.dma("sp", zp[s][:, 15:15 + CL], self.zT[j * 128:(j + 1) * 128, 0:CL], ("zp", s), ["zT"], [("zp", s)])
                self.dma("sp", zp[s][:, 301:301 + SEQ], self.zT[j * 128:(j + 1) * 128, CL:T], ("zp", s), ["zT"], [("zp", s)])
            load_chunk(0)
            for j in range(8):
                s = j % 2
                if j + 1 < 8:
                    load_chunk(j + 1)
                for k in range(31):
                    self.ts("dve" if k % 2 == 0 else "pool", dg[s][:, k, :], identb, self.ppc(l, "cdw", j * 31 + k), ALU.mult,
                            ["cstb", "pp"], [("dg", s, k)])
                for (c0, n, o0) in blocks:
                    pst = self.ps[psi % 4]
                    pk = ("ps", psi % 4)
                    psi += 1
                    for k in range(31):
                        self.mm(pst[:, 0:n], dg[s][:, k, :], zp[s][:, c0 + k:c0 + k + n], k == 0, k == 30,
                                [("dg", s, k), ("zp", s)], [pk])
                    self.act(zo[s][:, o0:o0 + n], pst[:, 0:n], AF.Identity, [pk, "pp"], [("zo", s)], bias=self.ppc(l, "cdb", j))
                self.dma("sp", self.zcT[j * 128:(j + 1) * 128, :], zo[s][:], ("zo", s), [("zo", s)], ["zcT"])

    def phase_rc(self, l):
        nc = self.nc
        WP = T + 6
        WO = T + 3
        WZ = 15 + CL + 30 + SEQ + 15
        with contextlib.ExitStack() as es:
            def sb(name, shape, dt):
                return es.enter_context(nc.sbuf_tensor("r%d_%s" % (l, name), list(shape), dt))
            rxp = [sb("rxp%d" % i, [128, WP], BF16) for i in range(2)]
            gel = [sb("gel%d" % i, [128, T], BF16) for i in range(2)]
            tro = [sb("tro0", [128, T], BF16)] * 2
            xr = sb("xr", [128, WO], F32)
            xrb = sb("xrb", [128, WO], BF16)
            rb = sb("r", [128, WO], F32)
            ib = sb("i", [128, WO], F32)
            tb = sb("t", [128, WO], F32)
            hf = sb("hf", [128, WO], F32)
            bd = sb("bd", [128, 8, 4, 128], BF16)
            dg4 = [sb("dg4_%d" % i, [128, 4, 128], BF16) for i in range(2)]
            zp = [sb("zp%d" % i, [128, WZ], BF16) for i in range(2)]
            zo = sb("zo", [128, T], BF16)
            dg31 = [sb("dg31_%d" % i, [128, 31, 128], BF16) for i in range(2)]
            for i in range(2):
                self.memset("pool", rxp[i][:], 0.0, [("rxp", i)])
                self.memset("pool", zp[i][:], 0.0, [("zp", i)])
            self.memset("pool", bd[:], 0.0, ["bd"])
            for d in range(2):
                for g, wsrc in enumerate((self.rnn_w_a, self.rnn_w_x)):
                    for half in range(2):
                        src = wsrc[l, d].rearrange("(j h) i o -> h i j o", h=2)[half]
                        self.dma("pool", bd[half * 64:(half + 1) * 64, :, d * 2 + g, half * 64:(half + 1) * 64], src,
                                 "bd", [], ["bd"])
            identb = self.cstb[:, 0, :]
            oblocks = [(c0, min(512, WO - c0)) for c0 in range(0, WO, 512)]
            cblocks = [(0, CL, 0)] + [(286 + 512 * i, 512, CL + 512 * i) for i in range(8)]
            cnt = {"c": 0, "g": 0}

            def load_chunk(j):
                s = j % 2
                self.dma("sp", rxp[s][:, 2:2 + CL], self.rxT[j * 128:(j + 1) * 128, 0:CL], ("rxp", s), ["rxT"], [("rxp", s)])
                self.dma("sp", rxp[s][:, 261:261 + SEQ], self.rxT[j * 128:(j + 1) * 128, CL:T], ("rxp", s), ["rxT"], [("rxp", s)])
                self.dma("sp", gel[s][:], self.rgT[j * 128:(j + 1) * 128, :], ("gel", s), ["rgT"], [("gel", s)])
                self.dma("sp", zp[s][:, 15:15 + CL], self.zT[j * 128:(j + 1) * 128, 0:CL], ("zp", s), ["zT"], [("zp", s)])
                self.dma("sp", zp[s][:, 301:301 + SEQ], self.zT[j * 128:(j + 1) * 128, CL:T], ("zp", s), ["zT"], [("zp", s)])

            def conv31_blocks(j, s, b0, b1):
                for bi in range(b0, b1):
                    c0, n, o0 = cblocks[bi]
                    b = cnt["c"] % 4
                    cnt["c"] += 1
                    pst, pk = self.ps[b], ("ps", b)
                    for k in range(31):
                        self.mm(pst[:, 0:n], dg31[s][:, k, :], zp[s][:, c0 + k:c0 + k + n], k == 0, k == 30,
                                [("dg31", s), ("zp", s)], [pk])
                    self.act(zo[:, o0:o0 + n], pst[:, 0:n], AF.Identity, [pk, "pp"], [("zo", bi)], bias=self.ppc(l, "cdb", j))

            def gates(j, d):
                for g, (dst, bname, dk) in enumerate(((rb, "rba", "r"), (ib, "rbx", "i"))):
                    for (c0, n) in oblocks:
                        b = 4 + cnt["g"] % 3
                        cnt["g"] += 1
                        pst, pk = self.ps[b], ("ps", b)
                        self.mm(pst[:, 0:n], bd[:, j, d * 2 + g, :], xrb[:, c0:c0 + n], True, True, ["bd", "xrb"], [pk])
                        self.act(dst[:, c0:c0 + n], pst[:, 0:n], AF.Sigmoid, [pk, "pp"], [dk], bias=self.ppc(l, bname, d * 8 + j))

            def scan(d):
                dst = hf if d == 0 else tb
                dk = "hf" if d == 0 else "t"
                if d == 0:
                    self.S.op("dve", lambda e: e.tensor_tensor_scan(
                        out=dst[:, 0:CL], data0=rb[:, 0:CL], data1=ib[:, 0:CL], initial=0.0, op0=ALU.mult, op1=ALU.add),
                        ["r", "i"], [dk])
                    self.S.op("dve", lambda e: e.tensor_tensor_scan(
                        out=dst[:, 259:WO], data0=rb[:, 259:WO], data1=ib[:, 259:WO], initial=dst[:, CL - 1:CL],
                        op0=ALU.mult, op1=ALU.add), ["r", "i", dk], [dk])
                else:
                    self.S.op("dve", lambda e: e.tensor_tensor_scan(
                        out=dst[:, 0:CL][:, ::-1], data0=rb[:, 0:CL][:, ::-1], data1=ib[:, 0:CL][:, ::-1], initial=0.0,
                        op0=ALU.mult, op1=ALU.add), ["r", "i", "t"], [dk])
                    self.S.op("dve", lambda e: e.tensor_tensor_scan(
                        out=dst[:, 259:WO][:, ::-1], data0=rb[:, 259:WO][:, ::-1], data1=ib[:, 259:WO][:, ::-1],
                        initial=dst[:, 0:1], op0=ALU.mult, op1=ALU.add), ["r", "i", dk], [dk])

            def build_dg31(j):
                for k in range(31):
                    self.ts("pool", dg31[j % 2][:, k, :], identb, self.ppc(l, "cdw", j * 31 + k), ALU.mult, ["cstb", "pp"],
                            [("dg31", j % 2)], s2=1.0, op1=ALU.mult)

            load_chunk(0)
            for j in range(8):
                s = j % 2
                if j + 1 < 8:
                    load_chunk(j + 1)
                for k in range(4):
                    self.ts("dve", dg4[s][:, k, :], identb, self.ppc(l, "rcw", j * 4 + k), ALU.mult, ["cstb", "pp"], [("dg4", s)])
                if j == 0:
                    build_dg31(0)
                if j + 1 < 8:
                    build_dg31(j + 1)
                for (c0, n) in oblocks:
                    b = 4 + cnt["g"] % 3
                    cnt["g"] += 1
                    pst, pk = self.ps[b], ("ps", b)
                    for k in range(4):
                        self.mm(pst[:, 0:n], dg4[s][:, k, :], rxp[s][:, c0 + k:c0 + k + n], k == 0, k == 3,
                                [("dg4", s), ("rxp", s)], [pk])
                    self.act(xr[:, c0:c0 + n], pst[:, 0:n], AF.Identity, [pk, "pp"], ["xr"], bias=self.ppc(l, "rcb", j))
                self.copy("dve", xrb[:], xr[:], ["xr"], ["xrb"])
                conv31_blocks(j, s, 0, 3)
                gates(j, 0)
                self.act(rb[:], rb[:], AF.Exp, ["r", ("cdec", l)], ["r"], scale=self.cdec[:, l, j:j + 1])
                self.tt("dve", tb[:], rb[:], rb[:], ALU.mult, ["r"], ["t"])
                self.tt("pool", ib[:], ib[:], xr[:], ALU.mult, ["i", "xr"], ["i"])
                conv31_blocks(j, s, 3, 5)
                self.act(tb[:], tb[:], AF.Sqrt, ["t"], ["t"], bias=1.0, scale=-1.0)
                self.tt("dve", ib[:], ib[:], tb[:], ALU.mult, ["i", "t"], ["i"])
                scan(0)
                conv31_blocks(j, s, 5, 9)
                self.dma("sp", self.zcT[j * 128:(j + 1) * 128, :], zo[:], "zo", [("zo", bi) for bi in range(9)],
                         [("zo", bi) for bi in range(9)] + ["zcT"])
                gates(j, 1)
                self.act(rb[:], rb[:], AF.Exp, ["r", ("cdec", l)], ["r"], scale=self.cdec[:, l, 8 + j:8 + j + 1])
                self.tt("dve", tb[:], rb[:], rb[:], ALU.mult, ["r"], ["t"])
                self.tt("pool", ib[:], ib[:], xr[:], ALU.mult, ["i", "xr"], ["i"])
                self.act(tb[:], tb[:], AF.Sqrt, ["t"], ["t"], bias=1.0, scale=-1.0)
                self.tt("dve", ib[:], ib[:], tb[:], ALU.mult, ["i", "t"], ["i"])
                scan(1)
                self.tt("pool", hf[:], hf[:], tb[:], ALU.add, ["hf", "t"], ["hf"])
                self.tt("dve", tro[s][:, 0:CL], hf[:, 0:CL], gel[s][:, 0:CL], ALU.mult, ["hf", ("gel", s)], ["tro"])
                self.tt("dve", tro[s][:, CL:T], hf[:, 259:WO], gel[s][:, CL:T], ALU.mult, ["hf", ("gel", s)], ["tro"])
                self.dma("sp", self.trT[j * 128:(j + 1) * 128, :], tro[s][:], "tro", ["tro"], ["trT"])

    def phase_attn_merge(self, l):
        with contextlib.ExitStack() as es:
            self.mw = [es.enter_context(self.nc.sbuf_tensor("mw%d_%d" % (l, i), [128, 8, D], BF16)) for i in range(4)]
            self.phase_attn(l)
            self.S.barrier()
            self.phase_merge(l)

    def phase_attn(self, l):
        nc = self.nc
        ctx_out = l < DEPTH - 1
        NKC = T // 128
        with contextlib.ExitStack() as es:
            def sb(name, shape, dt):
                return es.enter_context(nc.sbuf_tensor("d%d_%s" % (l, name), list(shape), dt))
            kh = [sb("kh%d" % i, [128, T], BF16) for i in range(2)]
            qh = [sb("qh%d" % i, [128, 2, T], BF16) for i in range(2)]
            vh = [sb("vh%d" % i, [128, NKC, 132], BF16) for i in range(2)]
            oh = [sb("oh%d" % i, [128, T], BF16) for i in range(2)]
            pT = [sb("pT%d" % i, [128, 2, 256], BF16) for i in range(4)]
            acs = [sb("acs%d" % i, [128, 2, 132], F32) for i in range(4)]
            o1 = [sb("o1_%d" % i, [128, 128], F32) for i in range(4)]
            o2 = [sb("o2_%d" % i, [128, 128], F32) for i in range(4)]
            on = [sb("on_%d" % i, [128, 128], BF16) for i in range(4)]
            junk = sb("junk", [128, 128], F32)
            sm = [sb("sm%d" % i, [128, 8], F32) for i in range(4)]
            for i in range(2):
                self.memset("pool", vh[i][:, :, 128:129], 1.0, [("vh", i)])
                self.memset("pool", qh[i][:], 0.0, [("qh", i)])
            identb = self.cstb[:, 0, :]
            st = {"sci": 0, "pti": 0, "ei": 0}
            pending = []
            for i, wsrc in enumerate((self.w_rnn_o, self.w_attn_o, self.w_conv_o, self.w_out)):
                self.dma("pool", self.wm_bf[i], wsrc[l], "precast", [], ["wm_bf"])
            for q in range(4):
                self.dma("pool", self.w1_bf[:, q * 1408:(q + 1) * 1408], self.w_ffn_in[l, :, q * 1408:(q + 1) * 1408], "precast", [], ["w1_bf"])
            for q in range(2):
                self.dma("pool", self.w2_bf[q * 1408:(q + 1) * 1408, :], self.w_ffn_out[l, q * 1408:(q + 1) * 1408, :], "precast", [], ["w2_bf"])

            def run_pending(step):
                keep = []
                for trig, fn in pending:
                    if step is None or step >= trig:
                        fn()
                    else:
                        keep.append((trig, fn))
                pending[:] = keep

            def load_head(h):
                s = h % 2
                kk, qk, vk = ("kh", s), ("qh", s), ("vh", s)
                self.dma("sp", kh[s][:], self.kT[h * 128:(h + 1) * 128, :], kk, ["KT"], [kk])
                self.dma("sp", qh[s][0:64, 0, :], self.qT[h * 128:h * 128 + 64, :], qk, ["QT"], [qk])
                self.dma("sp", qh[s][64:128, 1, :], self.qT[h * 128 + 64:(h + 1) * 128, :], qk, ["QT"], [qk])
                vsrc = self.Vs[:, h * 128:(h + 1) * 128].rearrange("(kc p) e -> p kc e", p=128)
                for k0 in range(0, NKC, 6):
                    k1 = min(NKC, k0 + 6)
                    self.dma("sp", vh[s][:, k0:k1, 0:128], vsrc[:, k0:k1, :], vk, ["Vs"], [vk])

            def make_stages(e, qs, q0, acc, acck, s, ok):
                smk = ("sm", e)

                def stage_a():
                    for m in range(2):
                        self.copy("dve", acs[e][:, m, 0:129], acc[m][qs][:, 0:129], [acck[m][qs]], [("acs", e, m)])

                def stage_b1():
                    self.recip(sm[e][:, 0:1], acs[e][:, 0, 128:129], [("acs", e, 0)], [(smk, 0)])
                    self.recip(sm[e][:, 1:2], acs[e][:, 1, 128:129], [("acs", e, 1)], [(smk, 1)])
                    self.ts("dve", sm[e][:, 2:3], sm[e][:, 1:2], self.nlam[:, l:l + 1], ALU.mult, [(smk, 1), ("nlam", l)], [(smk, 2)])
                    self.ts("dve", o1[e][:], acs[e][:, 0, 0:128], sm[e][:, 0:1], ALU.mult, [("acs", e, 0), (smk, 0)], [("o1", e)])
                    self.stt(o2[e][:], acs[e][:, 1, 0:128], sm[e][:, 2:3], o1[e][:], ALU.mult, ALU.add,
                             [("acs", e, 1), (smk, 2), ("o1", e)], [("o2", e)])
                    self.S.op("dve", lambda en: en.scalar_tensor_tensor(out=junk[:], in0=o2[e][:], scalar=1.0, in1=o2[e][:],
                                                                        op0=ALU.mult, op1=ALU.mult, accum_out=sm[e][:, 3:4]),
                              [("o2", e)], ["junk", (smk, 3)])

                def stage_b2():
                    self.act(sm[e][:, 4:5], sm[e][:, 3:4], AF.Ln, [(smk, 3)], [(smk, 4)], bias=EPS, scale=1.0 / 128.0)
                    self.act(sm[e][:, 5:6], sm[e][:, 4:5], AF.Exp, [(smk, 4)], [(smk, 5)], scale=-0.5)
                    self.ts("dve", on[e][:], o2[e][:], sm[e][:, 5:6], ALU.mult, [("o2", e), (smk, 5)], [("on", e)])

                def stage_b3():
                    self.S.op("pe", lambda en: en.transpose(out=self.psb[:, e * 128:(e + 1) * 128], in_=on[e][:], identity=identb),
                              [("on", e), "cstb"], [("psb", e)])
                    self.ts("dve", oh[s][:, q0 + qs * 128:q0 + (qs + 1) * 128], self.psb[:, e * 128:(e + 1) * 128],
                            self.gsub[:, l:l + 1], ALU.mult, [("psb", e), ("gsub", l)], [(ok, q0, qs)])
                return stage_a, stage_b1, stage_b2, stage_b3

            load_head(0)
            for h in range(8):
                s = h % 2
                kk, qk, vk, ok = ("kh", s), ("qh", s), ("vh", s), ("oh", s)
                if h + 1 < 8:
                    load_head(h + 1)
                if h == 3:
                    for i in range(4):
                        self.dma("sp", self.mw[i][:], self.wm_bf[i].rearrange("(kc p) n -> p kc n", p=128), ("mw", i),
                                 ["wm_bf"], [("mw", i)])
                qblocks = []
                if ctx_out:
                    qblocks.append((0, 2))
                for c0 in range(CL, T, 256):
                    qblocks.append((c0, NKC))
                okeys = []
                for (q0, nk) in qblocks:
                    acc = [[self.ps[3 + m * 2 + qs] for qs in range(2)] for m in range(2)]
                    acck = [[("ps", 3 + m * 2 + qs) for qs in range(2)] for m in range(2)]

                    def score(kc):
                        b = st["sci"] % 3
                        st["sci"] += 1
                        self.mm(self.ps[b][:, :].rearrange("p (m q) -> p m q", m=2), kh[s][:, kc * 128:(kc + 1) * 128],
                                qh[s][:, :, q0:q0 + 256], True, True, [kk, qk], [("ps", b)])
                        return b

                    def pv(kc, b):
                        pi = st["pti"] % 4
                        st["pti"] += 1
                        self.act(pT[pi][:].rearrange("p m q -> p (m q)"), self.ps[b][:, :], AF.Exp, [("ps", b)], [("pT", pi)],
                                 scale=0.125)
                        for m in range(2):
                            for qs in range(2):
                                self.mm(acc[m][qs][:, 0:129], pT[pi][:, m, qs * 128:(qs + 1) * 128], vh[s][:, kc, 0:129],
                                        kc == 0, kc == nk - 1, [("pT", pi), vk], [acck[m][qs]])
                    bq = []
                    for kc in range(nk):
                        bq.append((kc, score(kc)))
                        if len(bq) > 2:
                            pv(*bq.pop(0))
                        run_pending(kc)
                    while bq:
                        pv(*bq.pop(0))
                    run_pending(None)
                    for qs in range(2):
                        e = st["ei"] % 4
                        st["ei"] += 1
                        a_, b1_, b2_, b3_ = make_stages(e, qs, q0, acc, acck, s, ok)
                        a_()
                        pending.append((2, b1_))
                        pending.append((9, b2_))
                        pending.append((18, b3_))
                        okeys.append((ok, q0, qs))
                run_pending(None)
                if ctx_out:
                    self.dma("sp", self.oT[h * 128:(h + 1) * 128, :], oh[s][:], ok, okeys, ["oT"])
                else:
                    self.dma("sp", self.oT[h * 128:(h + 1) * 128, CL:T], oh[s][:, CL:T], ok, okeys, ["oT"])

    def load_sq_w(self, dst, src, key):
        self.dma("pool", dst[:], src.rearrange("(kc p) n -> p kc n", p=128), key, [], [key])

    def phase_merge(self, l):
        nc = self.nc
        last = l == DEPTH - 1
        NB = 256
        with contextlib.ExitStack() as es:
            def sb(name, shape, dt):
                return es.enter_context(nc.sbuf_tensor("e%d_%s" % (l, name), list(shape), dt))
            wr, wa, wc, wo = self.mw
            tr = [sb("tr%d" % i, [128, 8, NB], BF16) for i in range(2)]
            ot = [sb("ot%d" % i, [128, 8, NB], BF16) for i in range(2)]
            zc = [sb("zc%d" % i, [128, 8, 2, NB], BF16) for i in range(3)]
            gg = [sb("gg%d" % i, [128, 24, NB], BF16) for i in range(2)]
            hb = [sb("hb%d" % i, [128, 8, NB], F32) for i in range(2)]
            zl = [sb("zl%d" % i, [128, 8, NB], BF16) for i in range(2)]
            mu = sb("mu", [128, NB], F32)
            var = sb("var", [128, NB], F32)
            t1 = [sb("t1_%d" % i, [128, NB], F32) for i in range(2)]
            t2 = [sb("t2_%d" % i, [128, NB], F32) for i in range(2)]
            m1 = [sb("m1_%d" % i, [128, NB], F32) for i in range(2)]
            m2 = [sb("m2_%d" % i, [128, NB], F32) for i in range(2)]
            m3 = [sb("m3_%d" % i, [128, NB], F32) for i in range(2)]
            mm_ = sb("mm", [128, 8, NB], BF16)
            sq2 = sb("sq2", [128, 8, NB], BF16)
            tmp2 = sb("tmp2", [128, 8, NB], F32)
            rt2 = sb("rt2", [128, NB], F32)
            u2o = [sb("u2o%d" % i, [128, 8, NB], BF16) for i in range(2)]
            u2v = self.u2T.rearrange("(kc p) t -> p kc t", p=128)
            onesb = self.cstb[:, 2, :]
            blocks = self.tok_blocks(NB)
            if last:
                blocks = blocks[1:]
            psi = 0
            mi = 0

            def view(d, c0, n):
                return d.rearrange("(kc p) t -> p kc t", p=128)[:, :, c0:c0 + n]
            def load_a(bi):
                c0, n, v = blocks[bi]
                z = bi % 3
                self.dma("sp", zc[z][:, :, 0, :], view(self.zcT, c0, n), ("zc", z), ["zcT"], [("zc", z)])

            def prep(bi):
                c0, n, v = blocks[bi]
                s = bi % 2
                z = bi % 3
                self.dma("sp", tr[s][:], view(self.trT, c0, n), ("tr", s), ["trT"], [("tr", s)])
                self.dma("sp", ot[s][:], view(self.oT, c0, n), ("ot", s), ["oT"], [("ot", s)])
                self.dma("sp", gg[s][:], view(self.gT, c0, n), ("gg", s), ["gT"], [("gg", s)])
                self.dma("sp", hb[s][:], view(self.xin if l == 0 else self.hT, c0, n), ("hb", s),
                         ["hT"] + [("hb", s, oc) for oc in range(8)], [("hb", s, "ld")])
                self.act(zc[z][:, :, 1, :], zc[z][:, :, 0, :], AF.Square, [("zc", z)], [("zq", z)])
                pmq = self.ps[6]
                for kc in range(8):
                    self.mm(pmq[:, :].rearrange("p (a n) -> p a n", a=2), onesb, zc[z][:, kc, :, :], kc == 0, kc == 7,
                            [("zc", z), ("zq", z), "cstb"], [("ps", 6)])
                self.copy("act", mu[:], pmq[:, 0:NB], [("ps", 6)], ["mu"])
                self.tt("dve", var[:], mu[:], mu[:], ALU.mult, ["mu"], ["var"])
                self.tt("dve", var[:], pmq[:, NB:2 * NB], var[:], ALU.subtract, [("ps", 6), "var"], ["var"])
                self.ts("dve", var[:], var[:], 0.0, ALU.max, ["var"], ["var"])
                self.act(var[:], var[:], AF.Sqrt, ["var"], ["var"], bias=EPS)
                self.recip(var[:], var[:], ["var"], ["var"])
                for kc in range(8):
                    self.tt("dve", t1[kc % 2][:], zc[z][:, kc, 0, :], mu[:], ALU.subtract, [("zc", z), "mu"], [("t1", kc % 2)])
                    self.tt("pool", t2[kc % 2][:], t1[kc % 2][:], var[:], ALU.mult, [("t1", kc % 2), "var"], [("t2", kc % 2)])
                    self.act(zl[s][:, kc, :], t2[kc % 2][:], AF.Silu, [("t2", kc % 2), "pp"], [("zl", s, kc)], bias=self.ppc(l, "clb", kc),
                             scale=self.ppc(l, "clg", kc))
            load_a(0)
            if len(blocks) > 1:
                load_a(1)
            prep(0)
            for bi, (c0, n, v) in enumerate(blocks):
                s = bi % 2
                if bi + 2 < len(blocks):
                    load_a(bi + 2)
                if bi + 1 < len(blocks):
                    prep(bi + 1)
                for oc in range(8):
                    b0 = 3 * (oc % 2)
                    pr, pa, pc = self.ps[b0], self.ps[b0 + 1], self.ps[b0 + 2]
                    kr, ka, kc_ = ("ps", b0), ("ps", b0 + 1), ("ps", b0 + 2)
                    for kc in range(8):
                        self.mm(pr[:, 0:n], wr[:, kc, oc * 128:(oc + 1) * 128], tr[s][:, kc, :], kc == 0, kc == 7, ["wr", ("tr", s)], [kr])
                    for kc in range(8):
                        self.mm(pa[:, 0:n], wa[:, kc, oc * 128:(oc + 1) * 128], ot[s][:, kc, :], kc == 0, kc == 7, ["wa", ("ot", s)], [ka])
                    for kc in range(8):
                        self.mm(pc[:, 0:n], wc[:, kc, oc * 128:(oc + 1) * 128], zl[s][:, kc, :], kc == 0, kc == 7,
                                ["wc"] + [("zl", s, k2) for k2 in range(8)], [kc_])
                    e = mi % 2
                    mi += 1
                    self.tt("dve", m1[e][:], pr[:, 0:n], gg[s][:, oc, :], ALU.mult, [kr, ("gg", s)], [("m1", e)])
                    self.tt("dve", m2[e][:], pa[:, 0:n], gg[s][:, 8 + oc, :], ALU.mult, [ka, ("gg", s)], [("m2", e)])
                    self.tt("dve", m3[e][:], pc[:, 0:n], gg[s][:, 16 + oc, :], ALU.mult, [kc_, ("gg", s)], [("m3", e)])
                    self.tt("pool", m1[e][:], m1[e][:], m2[e][:], ALU.add, [("m1", e), ("m2", e)], [("m1", e)])
                    self.tt("pool", mm_[:, oc, :], m1[e][:], m3[e][:], ALU.add, [("m1", e), ("m3", e)], [("mm", oc)])
                for oc in range(8):
                    po = self.ps[oc % 6]
                    pok = ("ps", oc % 6)
                    for kc in range(8):
                        self.mm(po[:, 0:n], wo[:, kc, oc * 128:(oc + 1) * 128], mm_[:, kc, :], kc == 0, kc == 7,
                                ["wo"] + [("mm", k2) for k2 in range(8)], [pok])
                    self.stt(hb[s][:, oc, :], po[:, 0:n], self.mod[:, l, 16 + oc, v:v + 1], hb[s][:, oc, :], ALU.mult, ALU.add,
                             [pok, ("hb", s, "ld"), ("mod", l)], [("hb", s, oc)])
                ui = bi % 2
                self.norm_block(hb[s], [("hb", s, oc) for oc in range(8)], n, sq2, tmp2, rt2,
                                lambda kc, ui=ui, n=n: u2o[ui][:, kc, 0:n],
                                lambda kc, v: self.gsc2[:, l, kc, v:v + 1],
                                lambda kc, v: self.mod[:, l, 24 + kc, v:v + 1], v, "n2m", ("u2o", ui), offload=True)
                self.dma("sp", u2v[:, :, c0:c0 + n], u2o[ui][:, :, 0:n], ("u2o", ui), [("u2o", ui)], [("u2o", ui), "u2T"])
                self.dma("sp", view(self.hT, c0, n), hb[s][:], ("hb", s), [("hb", s, oc) for oc in range(8)], [("hb", s, "ld")])

    def phase_ffn(self, l):
        nc = self.nc
        last = l == DEPTH - 1
        NB = 512
        HF = DFF // 2
        NH = HF // 128
        hv = self.hT.rearrange("(kc p) t -> p kc t", p=128)
        uv = self.u2T.rearrange("(kc p) t -> p kc t", p=128)
        for half in range(2):
            if half:
                self.S.barrier()
            with contextlib.ExitStack() as es:
                def sb(name, shape, dt):
                    return es.enter_context(nc.sbuf_tensor("f%d%d_%s" % (l, half, name), list(shape), dt))
                w1a = sb("w1a", [128, 8, HF], BF16)
                w1b = sb("w1b", [128, 8, HF], BF16)
                w2 = sb("w2", [128, NH, D], BF16)
                hb3 = [sb("hb%d" % i, [128, 8, NB], F32) for i in range(3)]
                u2 = [sb("u2%d" % i, [128, 8, NB], BF16) for i in range(2)]
                si_ = [sb("si%d" % i, [128, NB], F32) for i in range(2)]
                ac = sb("ac", [128, NH, NB], BF16)
                if half == 1:
                    sq = sb("sq", [128, 8, NB], BF16)
                    tmp = sb("tmp", [128, 8, NB], F32)
                    rt = sb("rt", [128, NB], F32)
                    if last:
                        un = [sb("ob", [128, 8, NB], F32)]
                    else:
                        un = [sb("un%d" % i, [128, 8, NB], BF16) for i in range(2)]
                    u1v = self.u1T.rearrange("(kc p) t -> p kc t", p=128)
                    ov = self.out_d.rearrange("(kc p) t -> p kc t", p=128)
                self.dma("sp", w1a[:], self.w1_bf[:, half * HF:(half + 1) * HF].rearrange("(kc p) n -> p kc n", p=128),
                         "w1a", ["w1_bf"], ["w1a"])
                self.dma("sp", w1b[:], self.w1_bf[:, DFF + half * HF:DFF + (half + 1) * HF].rearrange("(kc p) n -> p kc n", p=128),
                         "w1b", ["w1_bf"], ["w1b"])
                self.dma("sp", w2[:], self.w2_bf[half * HF:(half + 1) * HF, :].rearrange("(kc p) n -> p kc n", p=128),
                         "w2", ["w2_bf"], ["w2"])
                blocks = self.tok_blocks(NB)
                if last:
                    blocks = blocks[1:]
                psi = 0
                ai = 0
                def prep(bi):
                    c0, n, v = blocks[bi]
                    s = bi % 2
                    h3 = bi % 3
                    self.dma("sp", hb3[h3][:, :, 0:n], hv[:, :, c0:c0 + n], ("hb", h3), ["hT"], [("hb", h3)])
                    self.dma("sp", u2[s][:, :, 0:n], uv[:, :, c0:c0 + n], ("u2", s), ["u2T"], [("u2", s)])
                prep(0)
                for bi, (c0, n, v) in enumerate(blocks):
                    s = bi % 2
                    h3 = bi % 3
                    if bi + 1 < len(blocks):
                        prep(bi + 1)
                    for fc in range(NH):
                        pa, pb = self.ps[psi % 4], self.ps[(psi + 1) % 4]
                        ka, kb = ("ps", psi % 4), ("ps", (psi + 1) % 4)
                        psi += 2
                        for kc in range(8):
                            self.mm(pa[:, 0:n], w1a[:, kc, fc * 128:(fc + 1) * 128], u2[s][:, kc, 0:n], kc == 0, kc == 7,
                                    ["w1a", ("u2", s)], [ka])
                        for kc in range(8):
                            self.mm(pb[:, 0:n], w1b[:, kc, fc * 128:(fc + 1) * 128], u2[s][:, kc, 0:n], kc == 0, kc == 7,
                                    ["w1b", ("u2", s)], [kb])
                        e = ai % 2
                        ai += 1
                        self.act(si_[e][:, 0:n], pa[:, 0:n], AF.Silu, [ka], [("si", e)])
                        self.tt("dve", ac[:, fc, 0:n], pb[:, 0:n], si_[e][:, 0:n], ALU.mult, [kb, ("si", e)], [("ac", fc)])
                    for oc in range(8):
                        po = self.ps[4 + oc % 2]
                        pk = ("ps", 4 + oc % 2)
                        for fc in range(NH):
                            self.mm(po[:, 0:n], w2[:, fc, oc * 128:(oc + 1) * 128], ac[:, fc, 0:n], fc == 0, fc == NH - 1,
                                    ["w2", ("ac", fc)], [pk])
                        self.stt(hb3[h3][:, oc, 0:n], po[:, 0:n], self.mod[:, l, 40 + oc, v:v + 1], hb3[h3][:, oc, 0:n], ALU.mult, ALU.add,
                                 [pk, ("hb", h3), ("mod", l)], [("hb", h3)])
                    if half == 1 and last:
                        self.norm_block(hb3[h3], ("hb", h3), n, sq, tmp, rt, lambda kc, n=n: un[0][:, kc, 0:n],
                                        lambda kc, v: self.gfin[:, kc:kc + 1], lambda kc, v: None, 0, "nf", "ob")
                        self.dma("sp", ov[:, :, c0 - CL:c0 - CL + n], un[0][:, :, 0:n], "ob", ["ob"], ["ob", "out"])
                    elif half == 1 and l + 1 < self.L:
                        ui = bi % 2
                        self.norm_block(hb3[h3], ("hb", h3), n, sq, tmp, rt, lambda kc, n=n, ui=ui: un[ui][:, kc, 0:n],
                                        lambda kc, v: self.gsc1[:, l + 1, kc, v:v + 1],
                                        lambda kc, v: self.mod[:, l + 1, 0 + kc, v:v + 1], v, "n1f", ("un", ui))
                        self.dma("sp", u1v[:, :, c0:c0 + n], un[ui][:, :, 0:n], ("un", ui), [("un", ui)], [("un", ui), "u1T"])
                    if not (half == 1 and last):
                        self.dma("sp", hv[:, :, c0:c0 + n], hb3[h3][:, :, 0:n], ("hb", h3), [("hb", h3)], ["hT"])

    def phase_final(self):
        nc = self.nc
        NB = 256
        with contextlib.ExitStack() as es:
            def sb(name, shape, dt):
                return es.enter_context(nc.sbuf_tensor("z_%s" % name, list(shape), dt))
            hb = [sb("hb%d" % i, [128, 8, NB], F32) for i in range(2)]
            ob = [sb("ob%d" % i, [128, 8, NB], F32) for i in range(2)]
            sq = sb("sq", [128, 8, NB], BF16)
            tmp = sb("tmp", [128, 8, NB], F32)
            rt = sb("rt", [128, NB], F32)
            hv = self.hT.rearrange("(kc p) t -> p kc t", p=128)
            ov = self.out_d.rearrange("(kc p) t -> p kc t", p=128)
            fblocks = self.tok_blocks(NB)[1:]
            self.dma("sp", hb[0][:], hv[:, :, fblocks[0][0]:fblocks[0][0] + NB], ("hb", 0), ["hT"], [("hb", 0)])
            for bi, (c0, n, v) in enumerate(fblocks):
                s = bi % 2
                if bi + 1 < len(fblocks):
                    c1 = fblocks[bi + 1][0]
                    self.dma("sp", hb[1 - s][:], hv[:, :, c1:c1 + NB], ("hb", 1 - s), ["hT"], [("hb", 1 - s)])
                self.norm_block(hb[s], ("hb", s), n, sq, tmp, rt, lambda kc, s=s: ob[s][:, kc, :],
                                lambda kc, v: self.gfin[:, kc:kc + 1], lambda kc, v: None, 0, "nf", ("ob", s))
                self.dma("sp", ov[:, :, c0 - CL:c0 - CL + n], ob[s][:], ("ob", s), [("ob", s)], ["out"])


def _rope_tables():
    rows = SEQ // 64
    r = np.repeat(np.arange(rows, dtype=np.float32), 64)
    col = np.tile(np.arange(64, dtype=np.float32), rows)
    inv = (np.float32(10000.0) ** (-np.arange(16, dtype=np.float32) / np.float32(16))).astype(np.float32)
    ang = np.concatenate([r[:, None] * inv, col[:, None] * inv], axis=-1).astype(np.float32)
    cos, sin = np.cos(ang).astype(np.float32), np.sin(ang).astype(np.float32)
    C = np.zeros((128, SEQ), np.float32)
    Sg = np.zeros((128, SEQ), np.float32)
    for p in range(128):
        d = p % 64
        jj = d % 32
        C[p] = cos[:, jj]
        Sg[p] = sin[:, jj] if d < 32 else -sin[:, jj]
    return C, Sg


def _consts():
    c = np.zeros((128, 3, 128), np.float32)
    c[:, 0, :] = np.eye(128, dtype=np.float32)
    for m in range(128):
        src = m + 32 if (m % 64) < 32 else m - 32
        c[src, 1, m] = 1.0
    c[:, 2, :] = 1.0 / 1024.0
    return c


def _pack_pp(inp):
    def chunks(vec):
        return np.ascontiguousarray(vec.reshape(-1, 128).T)
    pp = np.zeros((128, DEPTH, NPP), np.float32)
    for l in range(DEPTH):
        parts = {
            "b_in": chunks(inp["b_in"][l]),
            "g1": chunks(inp["g_norm1"][l]),
            "g2": chunks(inp["g_norm2"][l]),
            "rcw": inp["rnn_conv_w"][l].T.reshape(8, 128, 4).transpose(1, 0, 2).reshape(128, 32),
            "rcb": chunks(inp["rnn_conv_b"][l]),
            "rba": inp["rnn_b_a"][l].reshape(2, 8, 128).transpose(2, 0, 1).reshape(128, 16),
            "rbx": inp["rnn_b_x"][l].reshape(2, 8, 128).transpose(2, 0, 1).reshape(128, 16),
            "rlam": inp["rnn_lambda"][l].reshape(2, 8, 128).transpose(2, 0, 1).reshape(128, 16),
            "cdw": inp["conv_dw_w"][l].T.reshape(8, 128, 31).transpose(1, 0, 2).reshape(128, 248),
            "cdb": chunks(inp["conv_dw_b"][l]),
            "clg": chunks(inp["conv_ln_g"][l]),
            "clb": chunks(inp["conv_ln_b"][l]),
            "bmod": chunks(inp["b_mod"][l]),
            "gsub": inp["g_subln"][l].reshape(128, 1),
        }
        for name, n in PP_FIELDS:
            assert parts[name].shape == (128, n), (name, parts[name].shape)
            pp[:, l, PP_OFF[name]:PP_OFF[name] + n] = parts[name]
    return pp


def make_in_maps(inp):
    inp = {k: np.asarray(v) for k, v in inp.items()}
    C, Sg = _rope_tables()
    shared = {
        "pp": _pack_pp(inp),
        "gfin": np.ascontiguousarray(inp["g_final"].reshape(8, 128).T),
        "bv": np.ascontiguousarray(inp["b_in"][:, OFF_V:OFF_V + 1024]),
        "lq": np.ascontiguousarray(inp["lambda_qk"].reshape(1, DEPTH * 256)),
        "ropec": C, "ropes": Sg, "cst": _consts(),
    }
    for k in ("w_mod", "w_in", "rnn_w_a", "rnn_w_x", "w_rnn_o", "w_attn_o", "w_conv_o", "w_out", "w_ffn_in", "w_ffn_out"):
        shared[k] = np.ascontiguousarray(inp[k], dtype=np.float32)
    maps = []
    for b in range(NCORES):
        m = dict(shared)
        m["xin"] = np.ascontiguousarray(np.concatenate([inp["ctx"][b].T, inp["x"][b].T], axis=1), dtype=np.float32)
        cv = np.stack([inp["c"][b].reshape(8, 128).T, inp["c_ctx"].reshape(8, 128).T], axis=-1)
        m["cvec"] = np.ascontiguousarray(cv, dtype=np.float32)
        maps.append(m)
    return maps


_NC_CACHE = {}


def kernel(**inputs):
    if "nc" not in _NC_CACHE:
        _NC_CACHE["nc"] = Prog().build()
    nc = _NC_CACHE["nc"]
    maps = make_in_maps(inputs)
    res = run_bass_kernel_spmd(nc, maps, core_ids=list(range(NCORES)))
    out = np.stack([np.ascontiguousarray(res.results[b]["out"].T) for b in range(NCORES)], axis=0)
    return out.astype(np.float32)
```
